# Optimizing a Trainium2 kernel written in Bass

```python
import math
import jax, jax.numpy as jnp
from jax import lax
import numpy as np

D_MODEL = 1024
BATCH = 8
SEQ = 8192
DEPTH = 4

A_HEADS = 8
A_KV_HEADS = 2
A_HEAD_DIM = 64
WINDOW = 128
BLOCK = 128
B_HEADS = 8
Q_LORA = 384
KV_LORA = 256
NOPE_DIM = 64
ROPE_DIM = 32
V_DIM = 64
ROPE_THETA = 10000.0
N_BUCKETS = 32
MAX_DISTANCE = 128
D_FF = 4 * D_MODEL
EPS = 1e-5

A_Q_W = A_HEADS * A_HEAD_DIM
A_KV_W = A_KV_HEADS * A_HEAD_DIM
B_QK_DIM = NOPE_DIM + ROPE_DIM
A_OUT = A_HEADS * A_HEAD_DIM
B_OUT = B_HEADS * V_DIM
IN_SPLITS = (A_Q_W, A_KV_W, A_KV_W, Q_LORA, KV_LORA, ROPE_DIM, D_MODEL, D_MODEL)
D_IN = A_Q_W + 2 * A_KV_W + Q_LORA + KV_LORA + ROPE_DIM + 2 * D_MODEL

kernel_name = "hybrid_swa_sink_mla_gated_trunk"


def rmsnorm(x, g):
    xf = x.astype(jnp.float32)
    y = xf * lax.rsqrt(jnp.mean(xf * xf, axis=-1, keepdims=True) + EPS)
    return (y * g.astype(jnp.float32)).astype(x.dtype)


def rope(t, positions):
    half = ROPE_DIM // 2
    inv_freq = ROPE_THETA ** (-jnp.arange(half, dtype=jnp.float32) / half)
    ang = positions.astype(jnp.float32)[..., None] * inv_freq
    ang = ang.reshape(ang.shape[:2] + (1,) * (t.ndim - 3) + (half,))
    cos, sin = jnp.cos(ang), jnp.sin(ang)
    tf = t.astype(jnp.float32)
    t1, t2 = tf[..., :half], tf[..., half:]
    return jnp.concatenate([t1 * cos - t2 * sin, t2 * cos + t1 * sin], axis=-1).astype(t.dtype)


def t5_bucket(dist):
    max_exact = N_BUCKETS // 2
    n = jnp.maximum(dist, 0)
    nf = jnp.maximum(n, 1).astype(jnp.float32)
    large = max_exact + (jnp.log(nf / max_exact) / math.log(MAX_DISTANCE / max_exact)
                         * (N_BUCKETS - max_exact)).astype(jnp.int32)
    large = jnp.minimum(large, N_BUCKETS - 1)
    return jnp.where(n < max_exact, n, large)


def swa_sink_attention(q, k, v, sinks, rel_table):
    B, S = q.shape[:2]
    nb = S // BLOCK
    G = A_HEADS // A_KV_HEADS
    qb = q.reshape(B, nb, BLOCK, A_KV_HEADS, G, A_HEAD_DIM)

    def with_prev(t):
        tb = t.reshape(B, nb, BLOCK, A_KV_HEADS, A_HEAD_DIM)
        prev = jnp.pad(tb[:, :-1], ((0, 0), (1, 0), (0, 0), (0, 0), (0, 0)))
        return jnp.concatenate([prev, tb], axis=2)

    kb, vb = with_prev(k), with_prev(v)
    qi = jnp.arange(BLOCK)[:, None]
    kj = jnp.arange(2 * BLOCK)[None, :]
    dist = BLOCK + qi - kj
    in_win = (dist >= 0) & (dist < WINDOW)
    blk = jnp.arange(nb)[:, None, None]
    valid = in_win[None] & ((blk > 0) | (kj >= BLOCK)[None])

    bias = rel_table[t5_bucket(dist)]
    bias = bias.transpose(2, 0, 1).reshape(A_KV_HEADS, G, BLOCK, 2 * BLOCK).astype(jnp.float32)

    scale = 1.0 / math.sqrt(A_HEAD_DIM)
    s = jnp.einsum('bnqhgd,bnkhd->bnhgqk', qb, kb).astype(jnp.float32) * scale + bias
    s = jnp.where(valid[None, :, None, None], s, -jnp.inf)
    sink = sinks.astype(jnp.float32).reshape(A_KV_HEADS, G)[None, None, :, :, None, None]
    m = jnp.maximum(jnp.max(s, axis=-1, keepdims=True), sink)
    p = jnp.exp(s - m)
    denom = jnp.sum(p, axis=-1, keepdims=True) + jnp.exp(sink - m)
    p = (p / denom).astype(v.dtype)
    o = jnp.einsum('bnhgqk,bnkhd->bnqhgd', p, vb)
    return o.reshape(B, S, A_OUT)


def mla_attention(c_q, c_kv, k_rope, positions, q_norm, kv_norm, w_uq, w_ukv):
    B, S = c_q.shape[:2]
    q = (rmsnorm(c_q, q_norm) @ w_uq).reshape(B, S, B_HEADS, B_QK_DIM)
    q = jnp.concatenate([q[..., :NOPE_DIM], rope(q[..., NOPE_DIM:], positions)], axis=-1)
    kv = (rmsnorm(c_kv, kv_norm) @ w_ukv).reshape(B, S, B_HEADS, NOPE_DIM + V_DIM)
    k_nope, v = kv[..., :NOPE_DIM], kv[..., NOPE_DIM:]
    k_r = rope(k_rope, positions)
    k = jnp.concatenate([k_nope, jnp.broadcast_to(k_r[:, :, None, :], (B, S, B_HEADS, ROPE_DIM))], axis=-1)

    nb = S // BLOCK
    qb = q.reshape(B, nb, BLOCK, B_HEADS, B_QK_DIM).transpose(1, 0, 2, 3, 4)
    kpos = jnp.arange(S)
    scale = 1.0 / math.sqrt(B_QK_DIM)

    def attend(args):
        qblk, n = args
        s = jnp.einsum('bqhd,bkhd->bhqk', qblk, k).astype(jnp.float32) * scale
        qpos = n * BLOCK + jnp.arange(BLOCK)
        s = jnp.where(kpos[None, :] <= qpos[:, None], s, -jnp.inf)
        p = jax.nn.softmax(s, axis=-1).astype(v.dtype)
        return jnp.einsum('bhqk,bkhd->bqhd', p, v)

    o = lax.map(attend, (qb, jnp.arange(nb)))
    return o.transpose(1, 0, 2, 3, 4).reshape(B, S, B_OUT)


def setup_inputs(seed: int = 0) -> dict:
    key = jax.random.key(seed)
    ks = jax.random.split(key, 20)
    f32 = jnp.float32

    def nrm(k, shape, fan_in):
        return jax.random.normal(k, shape, f32) * (fan_in ** -0.5)

    def gain(k, shape):
        return 1.0 + 0.02 * jax.random.normal(k, shape, f32)

    x = jax.random.normal(ks[0], (BATCH, SEQ, D_MODEL), f32)
    offset = jax.random.randint(ks[1], (BATCH, 1), 0, 4096, dtype=jnp.int32)
    positions = (offset + jnp.arange(SEQ, dtype=jnp.int32)[None, :]).astype(jnp.int32)
    return {
        "x": x,
        "positions": positions,
        "rel_bias_table": 0.5 * jax.random.normal(ks[2], (N_BUCKETS, A_HEADS), f32),
        "norm_mix": gain(ks[3], (DEPTH, D_MODEL)),
        "w_in": nrm(ks[4], (DEPTH, D_MODEL, D_IN), D_MODEL),
        "attn_sinks": 0.5 * jax.random.normal(ks[5], (DEPTH, A_HEADS), f32),
        "q_norm": gain(ks[6], (DEPTH, Q_LORA)),
        "kv_norm": gain(ks[7], (DEPTH, KV_LORA)),
        "w_uq": nrm(ks[8], (DEPTH, Q_LORA, B_HEADS * B_QK_DIM), Q_LORA),
        "w_ukv": nrm(ks[9], (DEPTH, KV_LORA, B_HEADS * (NOPE_DIM + V_DIM)), KV_LORA),
        "w_branch_a": nrm(ks[10], (DEPTH, A_OUT, D_MODEL), A_OUT),
        "w_branch_b": nrm(ks[11], (DEPTH, B_OUT, D_MODEL), B_OUT),
        "w_out": nrm(ks[12], (DEPTH, D_MODEL, D_MODEL), D_MODEL),
        "norm_mlp": gain(ks[13], (DEPTH, D_MODEL)),
        "w_ff1": nrm(ks[14], (DEPTH, D_MODEL, D_FF), D_MODEL),
        "w_ff2": nrm(ks[15], (DEPTH, D_FF, D_MODEL), D_FF),
        "norm_final": gain(ks[16], (D_MODEL,)),
    }


def reference(x, positions, rel_bias_table, norm_mix, w_in, attn_sinks, q_norm, kv_norm,
              w_uq, w_ukv, w_branch_a, w_branch_b, w_out, norm_mlp, w_ff1, w_ff2, norm_final):
    split_at = [int(v) for v in np.cumsum(IN_SPLITS)[:-1]]
    for l in range(DEPTH):
        h = rmsnorm(x, norm_mix[l])
        z = h @ w_in[l]
        q_a, k_a, v_a, c_q, c_kv, k_rope, g_a, g_b = jnp.split(z, split_at, axis=-1)
        y_a = swa_sink_attention(q_a, k_a, v_a, attn_sinks[l], rel_bias_table) @ w_branch_a[l]
        y_b = mla_attention(c_q, c_kv, k_rope, positions, q_norm[l], kv_norm[l],
                            w_uq[l], w_ukv[l]) @ w_branch_b[l]
        merged = jax.nn.sigmoid(g_a) * y_a + jax.nn.sigmoid(g_b) * y_b
        x = x + merged @ w_out[l]
        h = rmsnorm(x, norm_mlp[l])
        x = x + jnp.square(jax.nn.relu(h @ w_ff1[l])) @ w_ff2[l]
    return rmsnorm(x, norm_final)
```

```python
import math
from contextlib import ExitStack

import numpy as np
import concourse.bass as bass
import concourse.mybir as mybir
from concourse.bass_utils import run_bass_kernel_spmd

F32 = mybir.dt.float32
BF16 = mybir.dt.bfloat16
I32 = mybir.dt.int32
AF = mybir.ActivationFunctionType
ALU = mybir.AluOpType

D = 1024
DFF = 4096
EPS = 1e-5
NCORES = 8
ENG = ("pe", "act", "dve", "pool", "sp")


class Prog:
    def __init__(self, nc, stack):
        self.nc = nc
        self.stack = stack
        self.sem = {}
        self.cnt = {}
        self.waited = {e: {} for e in ENG}
        self.q = {e: [] for e in ENG}
        for e in ("pe", "act", "dve", "pool"):
            self._mk(e)

    def _mk(self, key):
        self.sem[key] = self.stack.enter_context(self.nc.semaphore("s_" + key))
        self.cnt[key] = 0

    def _flat(self, deps, out):
        for d in deps:
            if d is None:
                continue
            if isinstance(d, list):
                self._flat(d, out)
            else:
                out.append(d)

    def _wait(self, e, deps):
        fl = []
        self._flat(deps, fl)
        mx = {}
        for d in fl:
            k, v = d[0], d[1]
            if k == e and (e == "pe" or len(d) == 3):
                continue
            if v > mx.get(k, 0):
                mx[k] = v
        for k, v in mx.items():
            if self.waited[e].get(k, 0) >= v:
                continue
            self.waited[e][k] = v
            self.q[e].append(("wait", k, v))

    def op(self, e, name, *args, deps=(), inc=True, **kw):
        self._wait(e, list(deps))
        self.q[e].append(("op", name, args, kw, inc))
        if inc:
            self.cnt[e] += 1
            return (e, self.cnt[e])
        return None

    def dma(self, e, out, in_, semkey, deps=(), **kw):
        if semkey not in self.sem:
            self._mk(semkey)
        self._wait(e, list(deps))
        self.q[e].append(("dma", out, in_, semkey, kw))
        self.cnt[semkey] += 16
        return (semkey, self.cnt[semkey])

    def barrier(self):
        allh = [(k, v) for k, v in self.cnt.items() if v > 0]
        for e in ENG:
            self._wait(e, allh)

    def flush(self):
        with self.nc.Block() as block:
            for e, reg in (("pe", block.tensor), ("act", block.scalar), ("dve", block.vector),
                           ("pool", block.gpsimd), ("sp", block.sync)):
                items = self.q[e]
                self.q[e] = []

                def run(eng, items=items, e=e):
                    for it in items:
                        if it[0] == "wait":
                            eng.wait_ge(self.sem[it[1]], it[2])
                        elif it[0] == "op":
                            _, name, args, kw, inc = it
                            ins = getattr(eng, name)(*args, **kw)
                            if inc:
                                ins.then_inc(self.sem[e], 1)
                        else:
                            _, out, in_, semkey, kw = it
                            eng.dma_start(out=out, in_=in_, **kw).then_inc(self.sem[semkey], 16)

                reg(run)


class Slot:
    def __init__(self, t, idx=0):
        self.t = t
        self.idx = idx
        self.rd = []
        self.wr = []
        self.pending = False

    def wdeps(self):
        d = [(h[0], h[1], "war") for h in self.rd if h is not None] + list(self.wr)
        self.rd = []
        self.wr = []
        return d

    def w(self, h):
        self.wr.append(h)
        self.pending = True

    def rdeps(self):
        return list(self.wr)

    def r(self, h):
        self.rd.append(h)
        self.pending = False


class Ring:
    def __init__(self, ts, check=False):
        self.s = [Slot(t, i) for i, t in enumerate(ts)]
        self.i = -1
        self.check = check

    def next(self):
        self.i = (self.i + 1) % len(self.s)
        if self.check:
            assert not self.s[self.i].pending, "ring slot reused before its consumer was emitted"
        return self.s[self.i]


def build_nc(S, L, dbg=False):
    T = S // 128
    NB = S // 512
    nc = bass.Bass("TRN2", target_bir_lowering=False)
    uid = [0]

    def nm(p="t"):
        uid[0] += 1
        return "%s%d" % (p, uid[0])

    def din(name, shape, dt=F32):
        return nc.dram_tensor(name, shape, dt, kind="ExternalInput").ap()

    def dscr(name, shape, dt):
        return nc.dram_tensor(name, shape, dt, kind="ExternalOutput" if dbg else "Internal").ap()

    x_in = din("x", [S, D])
    pos_row = din("pos_row", [1, S], I32)
    pos_tm = din("pos_tm", [128, T], I32)
    w_in = din("w_in", [L, D, 3488])
    w_uq = din("w_uq", [L, 384, 1024])
    w_ukv = din("w_ukv", [L, 256, 1024])
    w_a = din("w_a", [L, 512, D])
    w_b = din("w_b", [L, 512, D])
    w_o = din("w_o", [L, D, D])
    w_f1 = din("w_f1", [L, D, DFF])
    w_f2 = din("w_f2", [L, DFF, D])
    g_mix = din("g_mix", [L, D])
    g_mlp = din("g_mlp", [L, D])
    g_fin = din("g_fin", [1, D])
    g_q = din("g_q", [L, 384])
    g_kv = din("g_kv", [L, 256])
    sinks = din("sinks", [1, L * 8])
    bias_g = din("bias_g", [4, 128, 512])
    mask_g = din("mask_g", [4, 128, 512])
    ident = din("ident", [128, 128])
    tri = din("tri", [128, 128])
    invf_fm = din("invf_fm", [128, 1])
    out = nc.dram_tensor("out", [S, D], F32, kind="ExternalOutput").ap()

    xres = dscr("xres", [S, D], F32)
    hT_d = dscr("hT_d", [8, 128, S], BF16)
    OaT_d = dscr("OaT_d", [4, 128, S], BF16)
    ObT_d = dscr("ObT_d", [4, 128, S], BF16)
    cqT_d = dscr("cqT_d", [3, 128, S], BF16)
    ckT_d = dscr("ckT_d", [2, 128, S], BF16)
    krT_d = dscr("krT_d", [32, S], BF16)
    qnT_d = dscr("qnT_d", [4, 128, S], BF16)
    qrT_d = dscr("qrT_d", [2, 128, S], BF16)
    knT_d = dscr("knT_d", [4, 128, S], BF16)
    V_d = dscr("V_d", [8, S, 64], BF16)
    cosF_d = dscr("cosF_d", [128, S], F32)
    sinF_d = dscr("sinF_d", [128, S], F32)
    cosT_d = dscr("cosT_d", [128, T * 16], F32)
    sinT_d = dscr("sinT_d", [128, T * 16], F32)

    invf = (10000.0 ** (-(np.arange(16, dtype=np.float32)) / np.float32(16.0))).astype(np.float32)

    with ExitStack() as top:
        P = Prog(nc, top)

        def sbt(shape, dt, st=top):
            return st.enter_context(nc.sbuf_tensor(nm("sb"), shape, dt))

        def pst(shape, dt, st):
            return st.enter_context(nc.psum_tensor(nm("ps"), shape, dt))

        idb = sbt([128, 128], BF16)
        trib = sbt([128, 128], BF16)
        esink = sbt([128, L * 8], F32)
        const_h = []

        def trig(ang, ki, kf, mm, o_sin, o_cos, deps):
            C1 = 6.28125
            C2 = 2 * math.pi - 6.28125
            d = P.op("dve", "tensor_scalar", ki, ang, 1.0 / (2 * math.pi), None, ALU.mult, deps=deps)
            d = P.op("dve", "tensor_copy", kf, ki, deps=[d])
            d = P.op("dve", "scalar_tensor_tensor", ang, kf, -C1, ang, ALU.mult, ALU.add, deps=[d])
            d = P.op("dve", "scalar_tensor_tensor", ang, kf, -C2, ang, ALU.mult, ALU.add, deps=[d])
            d = P.op("dve", "tensor_scalar", mm, ang, math.pi, -2 * math.pi, ALU.is_gt, ALU.mult, deps=[d])
            d = P.op("dve", "tensor_tensor", ang, ang, mm, ALU.add, deps=[d])
            d = P.op("dve", "scalar_tensor_tensor", mm, ang, -1.0, ang, ALU.mult, ALU.max, deps=[d])
            a1 = P.op("act", "activation", o_sin, ang, AF.Sin, deps=[d])
            a2 = P.op("act", "activation", o_cos, mm, AF.Sin, bias=math.pi / 2, scale=-1.0, deps=[d])
            return [a1, a2]

        with ExitStack() as ph:
            CH = min(S, 2048)
            l = P.dma("pool", idb[:], ident, "c0")
            l = P.dma("pool", trib[:], tri, "c0")
            const_h.append(l)
            l3 = P.dma("sp", esink[:], sinks.partition_broadcast(128), "c2")
            a = P.op("act", "activation", esink[:], esink[:], AF.Exp, deps=[l3])
            const_h.append(a)
            pti = sbt([128, T], I32, ph)
            ptf = sbt([128, T], F32, ph)
            angT = sbt([128, T, 16], F32, ph)
            kiT = sbt([128, T, 16], I32, ph)
            kfT = sbt([128, T, 16], F32, ph)
            mmT = sbt([128, T, 16], F32, ph)
            l4 = P.dma("sp", pti[:], pos_tm, "c3")
            d = P.op("dve", "tensor_copy", ptf[:], pti[:], deps=[l4])
            dd = []
            for i in range(16):
                dd.append(P.op("dve", "tensor_scalar", angT[:, :, i], ptf[:], float(invf[i]), None, ALU.mult, deps=[d]))
            sinT0 = sbt([128, T, 16], F32, ph)
            cosT0 = sbt([128, T, 16], F32, ph)
            hh = trig(angT[:], kiT[:], kfT[:], mmT[:], sinT0[:], cosT0[:], dd)
            P.dma("pool", sinT_d, sinT0[:].rearrange("p t f -> p (t f)"), "c6", deps=hh)
            P.dma("pool", cosT_d, cosT0[:].rearrange("p t f -> p (t f)"), "c6", deps=hh)
            ivf = sbt([128, 1], F32, ph)
            l5 = P.dma("sp", ivf[:], invf_fm, "c4")
            pbi = sbt([128, CH], I32, ph)
            pbf = sbt([128, CH], F32, ph)
            ang = sbt([128, CH], F32, ph)
            kif = sbt([128, CH], I32, ph)
            kff = sbt([128, CH], F32, ph)
            mmf = sbt([128, CH], F32, ph)
            osn = sbt([128, CH], F32, ph)
            ocs = sbt([128, CH], F32, ph)
            prev = []
            for ch in range(S // CH):
                l6 = P.dma("sp", pbi[:], pos_row[0:1, ch * CH:(ch + 1) * CH].partition_broadcast(128), "c5", deps=prev)
                d = P.op("dve", "tensor_copy", pbf[:], pbi[:], deps=[l6])
                d = P.op("dve", "tensor_scalar", ang[:], pbf[:], ivf[:], None, ALU.mult, deps=[d, l5] + prev)
                hh = trig(ang[:], kif[:], kff[:], mmf[:], osn[:], ocs[:], [d] + prev)
                s1 = P.dma("pool", sinF_d[:, ch * CH:(ch + 1) * CH], osn[:], "c6", deps=hh)
                s2 = P.dma("pool", cosF_d[:, ch * CH:(ch + 1) * CH], ocs[:], "c6", deps=hh)
                prev = [s1, s2, d] + hh
            P.barrier()
            P.flush()

        def load_w(dst, src2d, ncols_lo, ncols_hi, nchunks, key):
            hs = []
            for c in range(nchunks):
                hs.append(P.dma("pool", dst[:, c, :], src2d[c * 128:(c + 1) * 128, ncols_lo:ncols_hi], key))
            return hs

        def rstd_ops(st, col, n, dep):
            a = P.op("act", "activation", st[:, col + 1:col + 2], st[:, col:col + 1], AF.Ln,
                     bias=EPS, scale=1.0 / n, deps=[dep])
            a = P.op("act", "activation", st[:, col + 1:col + 2], st[:, col + 1:col + 2], AF.Exp,
                     scale=-0.5, deps=[a])
            return a

        def phase_A1(l):
            xsrc = x_in if l == 0 else xres
            with ExitStack() as ph:
                wA = sbt([128, 8, 1440], BF16, ph)
                gmix = sbt([128, D], F32, ph)
                gq = sbt([128, 384], F32, ph)
                gkv = sbt([128, 256], F32, ph)
                xt = Ring([sbt([128, D], F32, ph) for _ in range(4)])
                junk = sbt([128, D], BF16, ph)
                stt = Ring([sbt([128, 8], F32, ph) for _ in range(8)])
                hb = Ring([sbt([128, D], BF16, ph) for _ in range(4)], check=True)
                hT = Ring([sbt([128, 8, 512], BF16, ph) for _ in range(2)])
                qaT = Ring([sbt([128, 4, 512], BF16, ph) for _ in range(2)])
                kaT = sbt([128, 128 + S], BF16, ph)
                va = sbt([128, T, 2, 65], BF16, ph)
                cqn = Ring([sbt([128, 384], BF16, ph) for _ in range(4)], check=True)
                ckn = Ring([sbt([128, 256], BF16, ph) for _ in range(4)], check=True)
                krb = Ring([sbt([128, 32], BF16, ph) for _ in range(4)], check=True)
                krs = sbt([128, 2, 16], F32, ph)
                tmpa = sbt([128, 2, 16], F32, ph)
                tmpb = sbt([128, 2, 16], F32, ph)
                cqT = Ring([sbt([128, 3, 512], BF16, ph) for _ in range(2)])
                ckT = Ring([sbt([128, 2, 512], BF16, ph) for _ in range(2)])
                krT = Ring([sbt([32, 512], BF16, ph) for _ in range(2)])
                pexp = Ring([sbt([128, 512], F32, ph) for _ in range(3)])
                pm = Ring([sbt([128, 512], BF16, ph) for _ in range(16)], check=True)
                oa = Ring([sbt([128, 512], BF16, ph) for _ in range(4)], check=True)
                oaT = Ring([sbt([128, 4, 512], BF16, ph) for _ in range(2)])
                den = Ring([sbt([128, 4], F32, ph) for _ in range(3)])
                tp = Slot(pst([128, 8, 128], BF16, ph))
                tp2 = Slot(pst([128, 8, 128], BF16, ph))
                smA = Slot(pst([128, 512], F32, ph))
                smB = Slot(pst([128, 512], F32, ph))
                fs = Ring([pst([128, 512], F32, ph) for _ in range(2)])
                oacc = [Slot(pst([128, 4, 65], F32, ph)) for _ in range(2)]

                E = sbt([128, 4, 512], F32, ph)
                mk = sbt([128, 4, 512], F32, ph)
                cosT = sbt([128, T, 16], F32, ph)
                sinT = sbt([128, T, 16], F32, ph)
                l1 = P.dma("sp", E[:], bias_g.rearrange("a p n -> p a n"), "c1a")
                l2 = P.dma("sp", mk[:], mask_g.rearrange("a p n -> p a n"), "c1b")
                a = P.op("act", "activation", E[:], E[:], AF.Exp, deps=[l1])
                d = P.op("dve", "tensor_tensor", E[:], E[:], mk[:], ALU.mult, deps=[a, l2])
                l3 = P.dma("sp", cosT[:].rearrange("p t f -> p (t f)"), cosT_d, "c1c")
                l4 = P.dma("sp", sinT[:].rearrange("p t f -> p (t f)"), sinT_d, "c1c")
                l3 = l4
                loc_h = [d, l3, l4]
                wl = load_w(wA, w_in[l], 0, 1440, 8, "wl0")
                gl = [P.dma("sp", gmix[:], g_mix[l:l + 1, :].partition_broadcast(128), "wl1"),
                      P.dma("sp", gq[:], g_q[l:l + 1, :].partition_broadcast(128), "wl1"),
                      P.dma("sp", gkv[:], g_kv[l:l + 1, :].partition_broadcast(128), "wl1")]
                vones = P.op("pool", "memset", va[:, :, :, 64:65], 1.0)
                ka_w = {}
                va_w = {}

                def block_gen(b):
                    hTs = hT.next()
                    hT_wd = hTs.wdeps()
                    cqTs = cqT.next()
                    cqT_wd = cqTs.wdeps()
                    ckTs = ckT.next()
                    ckT_wd = ckTs.wdeps()
                    krTs = krT.next()
                    krT_wd = krTs.wdeps()
                    bs = slice(b * 512, (b + 1) * 512)
                    stt_ = {}

                    def X(i):
                        t = 4 * b + i
                        xs = xt.next()
                        ld = P.dma("sp", xs.t[:], xsrc[t * 128:(t + 1) * 128, :], "xa%d" % xs.idx, deps=xs.wdeps())
                        xs.w(ld)
                        st = stt.next()
                        stw = st.wdeps()
                        a = P.op("act", "activation", junk[:], xs.t[:], AF.Square, accum_out=st.t[:, 0:1],
                                 deps=xs.rdeps() + stw)
                        xs.r(a)
                        a = rstd_ops(st.t, 0, D, a)
                        hs = hb.next()
                        d2 = P.op("dve", "scalar_tensor_tensor", hs.t[:], xs.t[:], st.t[:, 1:2], gmix[:],
                                  ALU.mult, ALU.mult, deps=[a] + gl + hs.wdeps())
                        xs.r(d2)
                        hs.w(d2)
                        st.w(d2)
                        stt_[i] = dict(st=st, hs=hs, d2=d2)

                    def Tr(i):
                        c0 = i * 128
                        hs = stt_[i]["hs"]
                        tpd = tp.wdeps()
                        for c in range(8):
                            pt = P.op("pe", "transpose", tp.t[:, c, :], hs.t[:, c * 128:(c + 1) * 128], idb[:],
                                      deps=hs.rdeps() + tpd + const_h, inc=(c == 7))
                        hs.r(pt)
                        tp.w(pt)
                        ev = P.op("act", "activation", hTs.t[:, :, c0:c0 + 128], tp.t[:], AF.Copy,
                                  deps=tp.rdeps() + hT_wd)
                        tp.r(ev)
                        hTs.w(ev)
                        stt_[i]["ev"] = ev

                    def M(i):
                        t = 4 * b + i
                        c0 = i * 128
                        st = stt_[i]["st"]
                        ev = stt_[i]["ev"]
                        d2 = stt_[i]["d2"]
                        wdA = smA.wdeps()
                        wdB = smB.wdeps()
                        for c in range(8):
                            mA = P.op("pe", "matmul", smA.t[:, 0:512], hTs.t[:, c, c0:c0 + 128], wA[:, c, 640:1152],
                                      start=(c == 0), stop=(c == 7), deps=[ev] + wl + wdA, inc=(c == 7))
                        for c in range(8):
                            mB = P.op("pe", "matmul", smB.t[:, 0:288], hTs.t[:, c, c0:c0 + 128], wA[:, c, 1152:1440],
                                      start=(c == 0), stop=(c == 7), deps=wdB, inc=(c == 7))
                        smA.w(mA)
                        smB.w(mB)
                        hTs.r(mB)
                        e1 = P.op("act", "activation", va[:, t, :, 0:64],
                                  smA.t[:, 0:128].rearrange("p (g d) -> p g d", g=2), AF.Copy, deps=[mA, vones])
                        smA.r(e1)
                        va_w[t] = e1
                        a3 = P.op("act", "activation", junk[:, 0:384], smA.t[:, 128:512], AF.Square,
                                  accum_out=st.t[:, 2:3], deps=[mA])
                        smA.r(a3)
                        a3 = rstd_ops(st.t, 2, 384, a3)
                        cq = cqn.next()
                        d4 = P.op("dve", "scalar_tensor_tensor", cq.t[:], smA.t[:, 128:512], st.t[:, 3:4], gq[:],
                                  ALU.mult, ALU.mult, deps=[a3, mA] + cq.wdeps())
                        smA.r(d4)
                        cq.w(d4)
                        a4 = P.op("act", "activation", junk[:, 512:768], smB.t[:, 0:256], AF.Square,
                                  accum_out=st.t[:, 4:5], deps=[mB])
                        smB.r(a4)
                        a4 = rstd_ops(st.t, 4, 256, a4)
                        ck = ckn.next()
                        d5 = P.op("dve", "scalar_tensor_tensor", ck.t[:], smB.t[:, 0:256], st.t[:, 5:6], gkv[:],
                                  ALU.mult, ALU.mult, deps=[a4, mB] + ck.wdeps())
                        smB.r(d5)
                        ck.w(d5)
                        st.r(d2)
                        st.r(d4)
                        st.r(d5)
                        d6 = P.op("dve", "tensor_copy", krs[:], smB.t[:, 256:288].rearrange("p (a b) -> p a b", a=2),
                                  deps=[mB])
                        smB.r(d6)
                        cosb = cosT[:, t, :].unsqueeze(1).to_broadcast([128, 2, 16])
                        sinb = sinT[:, t, :].unsqueeze(1).to_broadcast([128, 2, 16])
                        d7 = P.op("dve", "tensor_tensor", tmpa[:], krs[:], cosb, ALU.mult, deps=[d6] + loc_h)
                        d8 = P.op("dve", "tensor_tensor", tmpb[:], krs[:], sinb, ALU.mult, deps=[d6])
                        kr = krb.next()
                        krw = kr.wdeps()
                        d9 = P.op("dve", "tensor_tensor", kr.t[:, 0:16], tmpa[:, 0, :], tmpb[:, 1, :], ALU.subtract,
                                  deps=[d7, d8] + krw)
                        d10 = P.op("dve", "tensor_tensor", kr.t[:, 16:32], tmpa[:, 1, :], tmpb[:, 0, :], ALU.add,
                                   deps=[d7, d8])
                        kr.w(d9)
                        kr.w(d10)
                        stt_[i].update(cq=cq, ck=ck, kr=kr)

                    def U(i):
                        c0 = i * 128
                        cq, ck, kr = stt_[i]["cq"], stt_[i]["ck"], stt_[i]["kr"]
                        tp2d = tp2.wdeps()
                        for c in range(3):
                            P.op("pe", "transpose", tp2.t[:, c, :], cq.t[:, c * 128:(c + 1) * 128], idb[:],
                                 deps=cq.rdeps() + tp2d, inc=False)
                        for c in range(2):
                            P.op("pe", "transpose", tp2.t[:, 3 + c, :], ck.t[:, c * 128:(c + 1) * 128], idb[:],
                                 deps=ck.rdeps(), inc=False)
                        pt2 = P.op("pe", "transpose", tp2.t[0:32, 5, :], kr.t[:, 0:32], idb[:], deps=kr.rdeps())
                        cq.r(pt2)
                        ck.r(pt2)
                        kr.r(pt2)
                        tp2.w(pt2)
                        e2 = P.op("dve", "tensor_copy", cqTs.t[:, :, c0:c0 + 128], tp2.t[:, 0:3, :], deps=[pt2] + cqT_wd)
                        e3 = P.op("dve", "tensor_copy", ckTs.t[:, :, c0:c0 + 128], tp2.t[:, 3:5, :], deps=[pt2] + ckT_wd)
                        e4 = P.op("dve", "tensor_copy", krTs.t[0:32, c0:c0 + 128], tp2.t[0:32, 5, :], deps=[pt2] + krT_wd)
                        tp2.r(e4)
                        cqTs.w(e2)
                        ckTs.w(e3)
                        krTs.w(e4)

                    qa_ = {}

                    def FM(c4):
                        if "qas" not in qa_:
                            qa_["qas"] = qaT.next()
                            qa_["wd"] = qa_["qas"].wdeps()
                        qas = qa_["qas"]
                        fm = fs.next()
                        fd = fm.wdeps()
                        for c in range(8):
                            m = P.op("pe", "matmul", fm.t[:], wA[:, c, c4 * 128:(c4 + 1) * 128], hTs.t[:, c, :],
                                     start=(c == 0), stop=(c == 7), deps=hTs.rdeps() + fd, inc=(c == 7))
                        fm.w(m)
                        hTs.r(m)
                        if c4 < 4:
                            e = P.op("dve", "tensor_copy", qas.t[:, c4, :], fm.t[:], deps=[m] + qa_["wd"])
                            qas.w(e)
                        else:
                            e = P.op("act", "activation", kaT[:, 128 + b * 512:128 + (b + 1) * 512], fm.t[:], AF.Copy,
                                     deps=[m])
                            ka_w[b] = e
                        fm.r(e)

                    sw = {}

                    def QK(i, gs=(0, 1)):
                        t = 4 * b + i
                        qas = qa_["qas"]
                        tiles = ([t - 1] if t > 0 else []) + [t]
                        pmls = sw.setdefault(i, [])
                        for g in gs:
                            pml = []
                            for ti, kt in enumerate(tiles):
                                kind = 1 if kt == t else 0
                                sc = fs.next()
                                scd = sc.wdeps()
                                kdeps = [ka_w[kt // 4]]
                                m = P.op("pe", "matmul", sc.t[:].rearrange("p (c q) -> p c q", c=4),
                                         kaT[g * 64:(g + 1) * 64, 128 + kt * 128:128 + (kt + 1) * 128],
                                         qas.t[g * 64:(g + 1) * 64, :, i * 128:(i + 1) * 128],
                                         start=True, stop=True, deps=kdeps + qas.rdeps() + scd)
                                sc.w(m)
                                qas.r(m)
                                pe_ = pexp.next()
                                a = P.op("act", "activation", pe_.t[:], sc.t[:], AF.Exp, scale=0.125,
                                         deps=[m] + pe_.wdeps())
                                sc.r(a)
                                pe_.w(a)
                                pms = pm.next()
                                d = P.op("dve", "tensor_tensor", pms.t[:], pe_.t[:], E[:, g * 2 + kind, :], ALU.mult,
                                         deps=[a] + pms.wdeps() + loc_h)
                                pe_.r(d)
                                pms.w(d)
                                pml.append((pms, d, kt))
                            pmls.append(pml)

                    def PV(i):
                        pmls = sw[i]
                        oas = oa.next()
                        oa_wd = oas.wdeps()
                        for g in range(2):
                            pml = pmls[g]
                            oac = oacc[g]
                            oac_wd = oac.wdeps()
                            for c in range(4):
                                for ti, (pms, d, kt) in enumerate(pml):
                                    last = (c == 3 and ti == len(pml) - 1)
                                    mm = P.op("pe", "matmul", oac.t[:, c, :], pms.t[:, c * 128:(c + 1) * 128],
                                              va[:, kt, g, :], start=(ti == 0), stop=(ti == len(pml) - 1),
                                              deps=[d, va_w[kt]] + oac_wd, inc=last)
                            for (pms, d, kt) in pml:
                                pms.r(mm)
                            oac.w(mm)
                            dn = den.next()
                            d1 = P.op("dve", "tensor_tensor", dn.t[:], oac.t[:, :, 64],
                                      esink[:, l * 8 + g * 4:l * 8 + g * 4 + 4], ALU.add,
                                      deps=[mm] + dn.wdeps() + const_h)
                            d2 = P.op("dve", "reciprocal", dn.t[:], dn.t[:], deps=[d1])
                            d3 = P.op("dve", "tensor_tensor",
                                      oas.t[:, g * 256:(g + 1) * 256].rearrange("p (c d) -> p c d", c=4),
                                      oac.t[:, :, 0:64], dn.t[:].unsqueeze(2).to_broadcast([128, 4, 64]), ALU.mult,
                                      deps=[d2] + oa_wd)
                            dn.w(d2)
                            dn.r(d3)
                            oac.r(d3)
                            oas.w(d3)
                        sw[i] = oas

                    def TRo(i):
                        oas = sw[i]
                        if "oaTs" not in qa_:
                            qa_["oaTs"] = oaT.next()
                            qa_["oaT_wd"] = qa_["oaTs"].wdeps()
                        oaTs = qa_["oaTs"]
                        tp2d = tp2.wdeps()
                        for fc in range(4):
                            pt = P.op("pe", "transpose", tp2.t[:, fc, :], oas.t[:, fc * 128:(fc + 1) * 128], idb[:],
                                      deps=oas.rdeps() + tp2d, inc=(fc == 3))
                        oas.r(pt)
                        tp2.w(pt)
                        e = P.op("act", "activation", oaTs.t[:, :, i * 128:(i + 1) * 128], tp2.t[:, 0:4, :], AF.Copy,
                                 deps=[pt] + qa_["oaT_wd"])
                        tp2.r(e)
                        oaTs.w(e)

                    X(0)
                    X(1)
                    yield
                    Tr(0)
                    yield
                    M(0)
                    X(2)
                    yield
                    Tr(1)
                    yield
                    M(1)
                    X(3)
                    yield
                    U(0)
                    Tr(2)
                    yield
                    M(2)
                    yield
                    U(1)
                    Tr(3)
                    yield
                    M(3)
                    yield
                    U(2)
                    hTs.r(P.dma("pool", hT_d[:, :, bs].rearrange("c p s -> p c s"), hTs.t[:], "sa0_%d" % hTs.idx,
                                deps=hTs.rdeps()))
                    FM(0)
                    FM(1)
                    yield
                    U(3)
                    cqTs.r(P.dma("pool", cqT_d[:, :, bs].rearrange("c p s -> p c s"), cqTs.t[:], "sa1_%d" % cqTs.idx,
                                 deps=cqTs.rdeps()))
                    ckTs.r(P.dma("pool", ckT_d[:, :, bs].rearrange("c p s -> p c s"), ckTs.t[:], "sa2_%d" % ckTs.idx,
                                 deps=ckTs.rdeps()))
                    krTs.r(P.dma("pool", krT_d[:, bs], krTs.t[:], "sa3_%d" % krTs.idx, deps=krTs.rdeps()))
                    FM(2)
                    FM(3)
                    yield
                    FM(4)
                    QK(0, (0,))
                    yield
                    QK(0, (1,))
                    yield
                    QK(1, (0,))
                    PV(0)
                    yield
                    QK(1, (1,))
                    yield
                    QK(2, (0,))
                    PV(1)
                    TRo(0)
                    yield
                    QK(2, (1,))
                    yield
                    QK(3, (0,))
                    PV(2)
                    TRo(1)
                    yield
                    QK(3, (1,))
                    yield
                    PV(3)
                    TRo(2)
                    yield
                    TRo(3)
                    oaTs = qa_["oaTs"]
                    oaTs.r(P.dma("pool", OaT_d[:, :, bs].rearrange("c p s -> p c s"), oaTs.t[:], "sa4_%d" % oaTs.idx,
                                 deps=oaTs.rdeps()))

                NSTEP = 21
                active = []
                nextb = 0
                while active or nextb < NB:
                    if nextb < NB and (not active or (len(active) < 2 and active[0][1] >= NSTEP // 2)):
                        active.append([block_gen(nextb), 0])
                        nextb += 1
                    for g_ in list(active):
                        try:
                            next(g_[0])
                            g_[1] += 1
                        except StopIteration:
                            active.remove(g_)
                P.barrier()
                P.flush()

        def load_A2_w(l, st):
            wq = sbt([128, 3, 1024], BF16, st)
            wkv = sbt([128, 2, 1024], BF16, st)
            wl = load_w(wq, w_uq[l], 0, 1024, 3, "wl3")
            wl += load_w(wkv, w_ukv[l], 0, 1024, 2, "wl3")
            wn = []
            for c in range(3):
                v = wq[:, c, 768:1024].rearrange("p (h a b) -> p h a b", h=8, a=2)[:, :, 0, :]
                wn.append(P.op("pool", "tensor_scalar", v, v, -1.0, None, ALU.mult, deps=wl))
            return wq, wkv, wl + wn

        def load_E_w(l, st):
            wg = sbt([128, 8, 2048], BF16, st)
            wa = sbt([128, 4, D], BF16, st)
            wb = sbt([128, 4, D], BF16, st)
            wo = sbt([128, 8, D], BF16, st)
            wl = load_w(wg, w_in[l], 1440, 3488, 8, "wl4")
            wl += load_w(wa, w_a[l], 0, D, 4, "wl4")
            wl += load_w(wb, w_b[l], 0, D, 4, "wl4")
            wl += load_w(wo, w_o[l], 0, D, 8, "wl4")
            return wg, wa, wb, wo, wl

        def phase_A2(l, pre):
            with ExitStack() as ph:
                wq, wkv, wl = pre
                cqT = Ring([sbt([128, 3, 512], BF16, ph) for _ in range(2)])
                ckT = Ring([sbt([128, 2, 512], BF16, ph) for _ in range(2)])
                cosb = Ring([sbt([128, 512], F32, ph) for _ in range(2)])
                sinb = Ring([sbt([128, 512], F32, ph) for _ in range(2)])
                qn = Ring([sbt([128, 4, 512], BF16, ph) for _ in range(2)])
                qr = Ring([sbt([128, 2, 512], BF16, ph) for _ in range(2)])
                kn = Ring([sbt([128, 4, 512], BF16, ph) for _ in range(2)])
                vb = Ring([sbt([128, 4, 512], BF16, ph) for _ in range(2)])
                t1 = Ring([sbt([128, 512], F32, ph) for _ in range(2)])
                t2 = Ring([sbt([128, 512], F32, ph) for _ in range(2)])
                psr = Ring([pst([128, 512], F32, ph) for _ in range(8)])
                for b in range(NB):
                    bs = slice(b * 512, (b + 1) * 512)
                    cq = cqT.next()
                    ck = ckT.next()
                    cs = cosb.next()
                    sn = sinb.next()
                    l1 = P.dma("sp", cq.t[:], cqT_d[:, :, bs].rearrange("c p s -> p c s"), "la0_%d" % cq.idx, deps=cq.wdeps())
                    l2 = P.dma("sp", ck.t[:], ckT_d[:, :, bs].rearrange("c p s -> p c s"), "la1_%d" % ck.idx, deps=ck.wdeps())
                    l3 = P.dma("sp", cs.t[:], cosF_d[:, bs], "la2_%d" % cs.idx, deps=cs.wdeps())
                    l4 = P.dma("sp", sn.t[:], sinF_d[:, bs], "la3_%d" % sn.idx, deps=sn.wdeps())
                    cq.w(l1)
                    ck.w(l2)
                    cs.w(l3)
                    sn.w(l4)
                    qns = qn.next()
                    qn_wd = qns.wdeps()
                    for pr in range(4):
                        p_ = psr.next()
                        pd = p_.wdeps()
                        for c in range(3):
                            m = P.op("pe", "matmul", p_.t[:], wq[:, c, pr * 128:(pr + 1) * 128], cq.t[:, c, :],
                                     start=(c == 0), stop=(c == 2), deps=[l1] + wl + pd, inc=(c == 2))
                        p_.w(m)
                        cq.r(m)
                        e = P.op("act", "activation", qns.t[:, pr, :], p_.t[:], AF.Copy, deps=[m] + qn_wd)
                        p_.r(e)
                        qns.w(e)
                    qns.r(P.dma("pool", qnT_d[:, :, bs].rearrange("c p s -> p c s"), qns.t[:], "sb0_%d" % qns.idx,
                                deps=qns.rdeps()))
                    qrs = qr.next()
                    qr_wd = qrs.wdeps()
                    for gp in range(2):
                        pa = psr.next()
                        pad = pa.wdeps()
                        for c in range(3):
                            m1 = P.op("pe", "matmul", pa.t[:], wq[:, c, 512 + gp * 128:512 + (gp + 1) * 128], cq.t[:, c, :],
                                      start=(c == 0), stop=(c == 2), deps=pad, inc=(c == 2))
                        pa.w(m1)
                        pb = psr.next()
                        pbd = pb.wdeps()
                        for c in range(3):
                            m2 = P.op("pe", "matmul", pb.t[:], wq[:, c, 768 + gp * 128:768 + (gp + 1) * 128], cq.t[:, c, :],
                                      start=(c == 0), stop=(c == 2), deps=pbd, inc=(c == 2))
                        pb.w(m2)
                        cq.r(m2)
                        ta = t1.next()
                        tb = t2.next()
                        d1 = P.op("dve", "tensor_tensor", ta.t[:], pa.t[:], cs.t[:], ALU.mult, deps=[m1, l3] + ta.wdeps())
                        d2 = P.op("dve", "tensor_tensor", tb.t[:], pb.t[:], sn.t[:], ALU.mult, deps=[m2, l4] + tb.wdeps())
                        pa.r(d1)
                        pb.r(d2)
                        cs.r(d1)
                        sn.r(d2)
                        d3 = P.op("pool", "tensor_tensor", qrs.t[:, gp, :], ta.t[:], tb.t[:], ALU.add,
                                  deps=[d1, d2] + qr_wd)
                        ta.r(d3)
                        tb.r(d3)
                        qrs.w(d3)
                    qrs.r(P.dma("pool", qrT_d[:, :, bs].rearrange("c p s -> p c s"), qrs.t[:], "sb1_%d" % qrs.idx,
                                deps=qrs.rdeps()))
                    kns = kn.next()
                    kn_wd = kns.wdeps()
                    for pr in range(4):
                        p_ = psr.next()
                        pd = p_.wdeps()
                        for c in range(2):
                            m = P.op("pe", "matmul", p_.t[:], wkv[:, c, pr * 128:(pr + 1) * 128], ck.t[:, c, :],
                                     start=(c == 0), stop=(c == 1), deps=[l2] + pd, inc=(c == 1))
                        p_.w(m)
                        e = P.op("dve", "tensor_copy", kns.t[:, pr, :], p_.t[:], deps=[m] + kn_wd)
                        p_.r(e)
                        kns.w(e)
                    kns.r(P.dma("pool", knT_d[:, :, bs].rearrange("c p s -> p c s"), kns.t[:], "sb2_%d" % kns.idx,
                                deps=kns.rdeps()))
                    vbs = vb.next()
                    vb_wd = vbs.wdeps()
                    for i in range(4):
                        p_ = psr.next()
                        pd = p_.wdeps()
                        for c in range(2):
                            m = P.op("pe", "matmul", p_.t[:], ck.t[:, c, i * 128:(i + 1) * 128], wkv[:, c, 512:1024],
                                     start=(c == 0), stop=(c == 1), deps=pd, inc=(c == 1))
                        p_.w(m)
                        ck.r(m)
                        e = P.op("act", "activation", vbs.t[:, i, :], p_.t[:], AF.Copy, deps=[m] + vb_wd)
                        p_.r(e)
                        vbs.w(e)
                    vrd = vbs.rdeps()
                    for i in range(4):
                        t = 4 * b + i
                        vbs.r(P.dma("pool", V_d[:, t * 128:(t + 1) * 128, :].rearrange("h p d -> p h d"),
                                    vbs.t[:, i, :].rearrange("p (h d) -> p h d", h=8), "sb3_%d" % vbs.idx, deps=vrd))
                P.barrier()
                P.flush()

        def phase_C(l):
            with ExitStack() as ph:
                qT = [sbt([96, S], BF16, ph) for _ in range(2)]
                kT = [sbt([96, S], BF16, ph) for _ in range(2)]
                V = [sbt([128, T, 128], BF16, ph) for _ in range(2)]
                hslot = [Slot(None, i) for i in range(2)]
                ptr = Ring([sbt([128, 512], BF16, ph) for _ in range(6)])
                recr = Ring([sbt([64, 512], F32, ph) for _ in range(2)])
                onr = Ring([sbt([64, 512], BF16, ph) for _ in range(2)])
                sps = Ring([pst([128, 512], F32, ph) for _ in range(5)])
                ops_ = Ring([pst([128, 512], F32, ph) for _ in range(3)])
                vo = [P.op("pool", "memset", V[i][:, :, 64:128], 1.0) for i in range(2)]
                scale = 1.0 / math.sqrt(96.0)
                items = []
                for h in range(8):
                    for Q in range(NB):
                        nj = 4 * Q + 4
                        for j in range(nj):
                            items.append((h, Q, j, nj))
                hld = {}
                st1 = {}
                grp = {}
                pending_epi = []

                def head_load(h):
                    sl = hslot[h % 2]
                    wd = sl.wdeps()
                    k = "lc%d" % (h % 2)
                    q_, k_, v_ = qT[h % 2], kT[h % 2], V[h % 2]
                    P.dma("sp", q_[0:32, :], qrT_d[h // 4, (h % 4) * 32:(h % 4 + 1) * 32, :], k, deps=wd)
                    P.dma("sp", q_[32:96, :], qnT_d[h // 2, (h % 2) * 64:(h % 2 + 1) * 64, :], k)
                    P.dma("sp", k_[0:32, :], krT_d[:, :], k)
                    P.dma("sp", k_[32:96, :], knT_d[h // 2, (h % 2) * 64:(h % 2 + 1) * 64, :], k)
                    hh = P.dma("sp", v_[:, :, 0:64], V_d[h].rearrange("(t p) d -> p t d", p=128), k, deps=[vo[h % 2]])
                    sl.w(hh)
                    hld[h] = hh

                def stage1(i):
                    h, Q, j, nj = items[i]
                    r = j - 4 * Q if j >= 4 * Q else None
                    q0 = r * 128 if r else 0
                    s = sps.next()
                    m = P.op("pe", "matmul", s.t[:, q0:512], kT[h % 2][:, j * 128:(j + 1) * 128],
                             qT[h % 2][:, Q * 512 + q0:(Q + 1) * 512], start=True, stop=True,
                             deps=[hld[h]] + s.wdeps())
                    hslot[h % 2].r(m)
                    s.w(m)
                    p = ptr.next()
                    a = P.op("act", "activation", p.t[:, q0:512], s.t[:, q0:512], AF.Exp, scale=scale,
                             deps=[m] + p.wdeps())
                    s.r(a)
                    p.w(a)
                    if r is not None:
                        d = P.op("dve", "tensor_tensor", p.t[:, q0:q0 + 128], p.t[:, q0:q0 + 128], trib[:], ALU.mult,
                                 deps=[a] + const_h)
                        p.w(d)
                    st1[i] = (p, q0)

                def stage2(i):
                    h, Q, j, nj = items[i]
                    p, q0 = st1.pop(i)
                    if j == 0:
                        o = ops_.next()
                        grp[(h, Q)] = (o, o.wdeps())
                    o, owd = grp[(h, Q)]
                    mm = P.op("pe", "matmul", o.t[:, q0:512], V[h % 2][:, j, :], p.t[:, q0:512],
                              start=(j == 0), stop=(j == nj - 1), deps=p.rdeps() + owd)
                    p.r(mm)
                    hslot[h % 2].r(mm)
                    if j == nj - 1:
                        o.w(mm)
                        rc = recr.next()
                        on = onr.next()
                        d2 = P.op("dve", "reciprocal", rc.t[:], o.t[64:128, :], deps=[mm] + rc.wdeps())
                        d3 = P.op("dve", "tensor_tensor", on.t[:], o.t[0:64, :], rc.t[:], ALU.mult,
                                  deps=[d2] + on.wdeps())
                        rc.r(d3)
                        o.r(d3)
                        on.w(d3)
                        on.r(P.dma("pool", ObT_d[h // 2, (h % 2) * 64:(h % 2 + 1) * 64, Q * 512:(Q + 1) * 512], on.t[:],
                                   "sc%d" % on.idx, deps=[d3]))
                        del grp[(h, Q)]

                LOOK = 3
                n = len(items)
                head_load(0)
                head_load(1)
                for i in range(n + LOOK):
                    if i < n:
                        stage1(i)
                    if i - LOOK >= 0:
                        stage2(i - LOOK)
                        h_, Q_, j_, nj_ = items[i - LOOK]
                        if Q_ == NB - 1 and j_ == nj_ - 1 and h_ + 2 < 8:
                            head_load(h_ + 2)
                P.barrier()
                P.flush()

        def phase_E(l, pre):
            xsrc = x_in if l == 0 else xres
            with ExitStack() as ph:
                wg, wa, wb, wo, wl = pre
                hT = Ring([sbt([128, 8, 512], BF16, ph) for _ in range(2)])
                oaT = Ring([sbt([128, 4, 512], BF16, ph) for _ in range(2)])
                obT = Ring([sbt([128, 4, 512], BF16, ph) for _ in range(2)])
                xt = Ring([sbt([128, D], F32, ph) for _ in range(4)])
                sg = Ring([sbt([128, 512], F32, ph) for _ in range(4)])
                mr = Ring([sbt([128, 512], F32, ph) for _ in range(4)])
                mT = Ring([sbt([128, 8, 512], BF16, ph) for _ in range(2)])
                psg = Ring([pst([128, 512], F32, ph) for _ in range(3)])
                psy = Ring([pst([128, 512], F32, ph) for _ in range(3)])
                pso = Ring([pst([128, 512], F32, ph) for _ in range(2)])
                for b in range(NB):
                    bs = slice(b * 512, (b + 1) * 512)
                    hs = hT.next()
                    oas = oaT.next()
                    obs = obT.next()
                    l1 = P.dma("sp", hs.t[:], hT_d[:, :, bs].rearrange("c p s -> p c s"), "le0_%d" % hs.idx, deps=hs.wdeps())
                    l2 = P.dma("sp", oas.t[:], OaT_d[:, :, bs].rearrange("c p s -> p c s"), "le1_%d" % oas.idx, deps=oas.wdeps())
                    l3 = P.dma("sp", obs.t[:], ObT_d[:, :, bs].rearrange("c p s -> p c s"), "le2_%d" % obs.idx, deps=obs.wdeps())
                    hs.w(l1)
                    oas.w(l2)
                    obs.w(l3)
                    ms = mT.next()
                    m_wd = ms.wdeps()
                    for dc in range(8):
                        halves = []
                        for br, (wbr, oT, lo, goff) in enumerate(((wa, oas, l2, 0), (wb, obs, l3, 1024))):
                            pg = psg.next()
                            pgd = pg.wdeps()
                            for c in range(8):
                                m1 = P.op("pe", "matmul", pg.t[:], wg[:, c, goff + dc * 128:goff + (dc + 1) * 128],
                                          hs.t[:, c, :], start=(c == 0), stop=(c == 7), deps=[l1] + wl + pgd,
                                          inc=(c == 7))
                            pg.w(m1)
                            py = psy.next()
                            pyd = py.wdeps()
                            for c in range(4):
                                m2 = P.op("pe", "matmul", py.t[:], wbr[:, c, dc * 128:(dc + 1) * 128], oT.t[:, c, :],
                                          start=(c == 0), stop=(c == 3), deps=[lo] + pyd, inc=(c == 3))
                            py.w(m2)
                            hs.r(m1)
                            oT.r(m2)
                            s_ = sg.next()
                            a = P.op("act", "activation", s_.t[:], pg.t[:], AF.Sigmoid, deps=[m1] + s_.wdeps())
                            pg.r(a)
                            s_.w(a)
                            m_ = mr.next()
                            d = P.op("dve", "tensor_tensor", m_.t[:], s_.t[:], py.t[:], ALU.mult,
                                     deps=[a, m2] + m_.wdeps())
                            s_.r(d)
                            py.r(d)
                            m_.w(d)
                            halves.append((m_, d))
                        (ma, da), (mb_, db) = halves
                        d3 = P.op("pool", "tensor_tensor", ms.t[:, dc, :], ma.t[:], mb_.t[:], ALU.add,
                                  deps=[da, db] + m_wd)
                        ma.r(d3)
                        mb_.r(d3)
                        ms.w(d3)
                    for i in range(4):
                        t = 4 * b + i
                        xs = xt.next()
                        lx = P.dma("sp", xs.t[:], xsrc[t * 128:(t + 1) * 128, :], "le3_%d" % xs.idx, deps=xs.wdeps())
                        xs.w(lx)
                        dh = []
                        for hf in range(2):
                            po = pso.next()
                            pod = po.wdeps()
                            for c in range(8):
                                m = P.op("pe", "matmul", po.t[:], ms.t[:, c, i * 128:(i + 1) * 128],
                                         wo[:, c, hf * 512:(hf + 1) * 512], start=(c == 0), stop=(c == 7),
                                         deps=ms.rdeps() + pod, inc=(c == 7))
                            po.w(m)
                            ms.r(m)
                            d = P.op("dve", "tensor_tensor", xs.t[:, hf * 512:(hf + 1) * 512],
                                     xs.t[:, hf * 512:(hf + 1) * 512], po.t[:], ALU.add, deps=[m, lx])
                            po.r(d)
                            dh.append(d)
                        xs.r(P.dma("pool", xres[t * 128:(t + 1) * 128, :], xs.t[:], "se0_%d" % xs.idx, deps=dh))
                P.barrier()
                P.flush()

        def phase_F(l, last):
            with ExitStack() as ph:
                w1 = sbt([128, 8, DFF], BF16, ph)
                w2 = sbt([128, 32, D], BF16, ph)
                gm = sbt([128, D], F32, ph)
                gf = sbt([128, D], F32, ph) if last else None
                xt = Ring([sbt([128, D], F32, ph) for _ in range(3)])
                junk = sbt([128, D], BF16, ph)
                stt = Ring([sbt([128, 8], F32, ph) for _ in range(4)])
                hb = Ring([sbt([128, D], BF16, ph) for _ in range(2)])
                h2T = Ring([sbt([128, 8, 512], BF16, ph) for _ in range(2)])
                uT = Ring([sbt([128, 32, 512], BF16, ph) for _ in range(1)])
                rl = Ring([sbt([128, 512], F32, ph) for _ in range(2)])
                tp = Slot(pst([128, 8, 128], BF16, ph))
                pu = Ring([pst([128, 512], F32, ph) for _ in range(4)])
                pd_ = Ring([pst([128, 512], F32, ph) for _ in range(3)])
                wl = load_w(w1, w_f1[l], 0, DFF, 8, "wl0")
                wl2 = load_w(w2, w_f2[l], 0, D, 32, "wl2")
                gl = [P.dma("sp", gm[:], g_mlp[l:l + 1, :].partition_broadcast(128), "wl1")]
                if last:
                    gl.append(P.dma("sp", gf[:], g_fin.partition_broadcast(128), "wl1"))
                blk = {}

                def norm_tile(b, i):
                    if i == 0:
                        hs2 = h2T.next()
                        blk[b] = (hs2, hs2.wdeps(), {})
                    t = 4 * b + i
                    xs = xt.next()
                    lx = P.dma("sp", xs.t[:], xres[t * 128:(t + 1) * 128, :], "lf0_%d" % xs.idx, deps=xs.wdeps())
                    xs.w(lx)
                    st = stt.next()
                    a = P.op("act", "activation", junk[:], xs.t[:], AF.Square, accum_out=st.t[:, 0:1],
                             deps=[lx] + st.wdeps())
                    xs.r(a)
                    a = rstd_ops(st.t, 0, D, a)
                    hs = hb.next()
                    d2 = P.op("dve", "scalar_tensor_tensor", hs.t[:], xs.t[:], st.t[:, 1:2], gm[:],
                              ALU.mult, ALU.mult, deps=[a, lx] + gl + hs.wdeps())
                    xs.r(d2)
                    st.r(d2)
                    hs.w(d2)
                    blk[b][2][i] = hs

                def tr_tile(b, i):
                    hs2, h2_wd, hss = blk[b]
                    hs = hss[i]
                    tpd = tp.wdeps()
                    for c in range(8):
                        pt = P.op("pe", "transpose", tp.t[:, c, :], hs.t[:, c * 128:(c + 1) * 128], idb[:],
                                  deps=hs.rdeps() + tpd + const_h, inc=(c == 7))
                    hs.r(pt)
                    tp.w(pt)
                    ev = P.op("act", "activation", hs2.t[:, :, i * 128:(i + 1) * 128], tp.t[:], AF.Copy,
                              deps=[pt] + h2_wd)
                    tp.r(ev)
                    hs2.w(ev)

                for i in range(4):
                    norm_tile(0, i)
                    tr_tile(0, i)
                for b in range(NB):
                    hs2 = blk[b][0]
                    us = uT.next()
                    u_wd = us.wdeps()
                    uw = {}
                    for fc in range(32):
                        p_ = pu.next()
                        ppd = p_.wdeps()
                        for c in range(8):
                            m = P.op("pe", "matmul", p_.t[:], w1[:, c, fc * 128:(fc + 1) * 128], hs2.t[:, c, :],
                                     start=(c == 0), stop=(c == 7), deps=hs2.rdeps() + wl + ppd, inc=(c == 7))
                        p_.w(m)
                        hs2.r(m)
                        r_ = rl.next()
                        a = P.op("act", "activation", r_.t[:], p_.t[:], AF.Relu, deps=[m] + r_.wdeps())
                        p_.r(a)
                        r_.w(a)
                        eng = "pool" if fc % 2 == 0 else "dve"
                        d = P.op(eng, "tensor_tensor", us.t[:, fc, :], r_.t[:], r_.t[:], ALU.mult, deps=[a] + u_wd)
                        r_.r(d)
                        us.w(d)
                        uw[fc] = d
                        if b + 1 < NB and fc == 16:
                            norm_tile(b + 1, 0)
                        if b + 1 < NB and fc == 28:
                            norm_tile(b + 1, 1)
                    for i in range(4):
                        t = 4 * b + i
                        xs = xt.next()
                        lx = P.dma("sp", xs.t[:], xres[t * 128:(t + 1) * 128, :], "lf1_%d" % xs.idx, deps=xs.wdeps())
                        xs.w(lx)
                        dh = []
                        for hf in range(2):
                            k_ = 2 * i + hf
                            if b + 1 < NB:
                                if k_ in (0, 2, 4, 6):
                                    tr_tile(b + 1, k_ // 2)
                                if k_ == 1:
                                    norm_tile(b + 1, 2)
                                if k_ == 3:
                                    norm_tile(b + 1, 3)
                            po = pd_.next()
                            pod = po.wdeps()
                            for fc in range(32):
                                m = P.op("pe", "matmul", po.t[:], us.t[:, fc, i * 128:(i + 1) * 128],
                                         w2[:, fc, hf * 512:(hf + 1) * 512], start=(fc == 0), stop=(fc == 31),
                                         deps=[uw[fc]] + wl2 + pod, inc=(fc == 31))
                            po.w(m)
                            us.r(m)
                            d = P.op("dve", "tensor_tensor", xs.t[:, hf * 512:(hf + 1) * 512],
                                     xs.t[:, hf * 512:(hf + 1) * 512], po.t[:], ALU.add, deps=[m, lx])
                            po.r(d)
                            dh.append(d)
                        if not last:
                            xs.r(P.dma("pool", xres[t * 128:(t + 1) * 128, :], xs.t[:], "sf0_%d" % xs.idx, deps=dh))
                        else:
                            st = stt.next()
                            a = P.op("act", "activation", junk[:], xs.t[:], AF.Square, accum_out=st.t[:, 0:1],
                                     deps=dh + st.wdeps())
                            a = rstd_ops(st.t, 0, D, a)
                            d2 = P.op("dve", "scalar_tensor_tensor", xs.t[:], xs.t[:], st.t[:, 1:2], gf[:],
                                      ALU.mult, ALU.mult, deps=[a] + gl)
                            st.r(d2)
                            xs.r(P.dma("pool", out[t * 128:(t + 1) * 128, :], xs.t[:], "sf0_%d" % xs.idx, deps=[d2]))
                P.barrier()
                P.flush()

        for l in range(L):
            with ExitStack() as sc1:
                pre = load_A2_w(l, sc1)
                phase_A1(l)
                phase_A2(l, pre)
            with ExitStack() as sc2:
                pre = load_E_w(l, sc2)
                phase_C(l)
                phase_E(l, pre)
            phase_F(l, l == L - 1)
    return nc


def _t5_bucket_np(dist):
    n = np.maximum(dist, 0)
    nf = np.maximum(n, 1).astype(np.float32)
    large = 16 + (np.log(nf / np.float32(16)) / np.float32(math.log(128 / 16)) * np.float32(16)).astype(np.int32)
    large = np.minimum(large, 31)
    return np.where(n < 16, n, large)


def prep_inputs(S, L, x, positions, rel_bias_table, norm_mix, w_in, attn_sinks, q_norm, kv_norm,
                w_uq, w_ukv, w_branch_a, w_branch_b, w_out, norm_mlp, w_ff1, w_ff2, norm_final):
    f = lambda a: np.ascontiguousarray(np.asarray(a, dtype=np.float32))
    w_in = f(w_in)
    perm = []
    for c in range(4):
        perm += list(range(c * 64, (c + 1) * 64)) + list(range((4 + c) * 64, (5 + c) * 64))
    cols = np.array(perm + list(range(512, 3488)))
    w_in_p = np.ascontiguousarray(w_in[:, :, cols])
    w_uq = f(w_uq)
    nope = np.concatenate([np.arange(h * 96, h * 96 + 64) for h in range(8)])
    rope = np.concatenate([np.arange(h * 96 + 64, h * 96 + 96) for h in range(8)])
    rot = np.concatenate([np.concatenate([np.arange(h * 96 + 80, h * 96 + 96), np.arange(h * 96 + 64, h * 96 + 80)])
                          for h in range(8)])
    w_uq_p = np.ascontiguousarray(w_uq[:, :, np.concatenate([nope, rope, rot])])
    w_ukv = f(w_ukv)
    kn = np.concatenate([np.arange(h * 128, h * 128 + 64) for h in range(8)])
    vv = np.concatenate([np.arange(h * 128 + 64, h * 128 + 128) for h in range(8)])
    w_ukv_p = np.ascontiguousarray(w_ukv[:, :, np.concatenate([kn, vv])])
    tab = f(rel_bias_table)
    kk = np.arange(128)[:, None]
    qq = np.arange(128)[None, :]
    bias_g = np.zeros((4, 128, 4, 128), np.float32)
    mask_g = np.zeros((4, 128, 4, 128), np.float32)
    for g in range(2):
        for tile in range(2):
            dist = (128 + qq - kk) if tile == 0 else (qq - kk)
            valid = (dist >= 0) & (dist < 128)
            bk = _t5_bucket_np(dist)
            for c in range(4):
                bias_g[g * 2 + tile, :, c, :] = tab[bk, g * 4 + c]
                mask_g[g * 2 + tile, :, c, :] = valid
    common = {
        "w_in": w_in_p, "w_uq": w_uq_p, "w_ukv": w_ukv_p,
        "w_a": f(w_branch_a), "w_b": f(w_branch_b), "w_o": f(w_out), "w_f1": f(w_ff1), "w_f2": f(w_ff2),
        "g_mix": f(norm_mix), "g_mlp": f(norm_mlp), "g_fin": f(norm_final).reshape(1, D),
        "g_q": f(q_norm), "g_kv": f(kv_norm), "sinks": f(attn_sinks).reshape(1, L * 8),
        "bias_g": bias_g.reshape(4, 128, 512), "mask_g": mask_g.reshape(4, 128, 512),
        "ident": np.eye(128, dtype=np.float32),
        "tri": (kk <= qq).astype(np.float32),
        "invf_fm": (10000.0 ** (-((np.arange(128) % 16).astype(np.float32)) / np.float32(16.0))).astype(np.float32).reshape(128, 1),
    }
    x = f(x)
    positions = np.asarray(positions).astype(np.int32)
    maps = []
    for b in range(x.shape[0]):
        m = dict(common)
        m["x"] = np.ascontiguousarray(x[b])
        m["pos_row"] = np.ascontiguousarray(positions[b].reshape(1, S))
        m["pos_tm"] = np.ascontiguousarray(positions[b].reshape(S // 128, 128).T)
        maps.append(m)
    return maps


_NC_CACHE = {}


def kernel(**inputs):
    x = np.asarray(inputs["x"])
    B, S, _ = x.shape
    L = np.asarray(inputs["w_in"]).shape[0]
    maps = prep_inputs(S, L, **inputs)
    key = (S, L)
    if key not in _NC_CACHE:
        _NC_CACHE[key] = build_nc(S, L)
    nc = _NC_CACHE[key]
    res = run_bass_kernel_spmd(nc, maps, core_ids=list(range(B)))
    return np.stack([np.asarray(r["out"], dtype=np.float32) for r in res.results], axis=0)
```

```python
import math
from contextlib import ExitStack

import numpy as np
import concourse.bass as bass
import concourse.mybir as mybir
from concourse.bass_utils import run_bass_kernel_spmd

F32 = mybir.dt.float32
BF16 = mybir.dt.bfloat16
I32 = mybir.dt.int32
AF = mybir.ActivationFunctionType
ALU = mybir.AluOpType

D = 1024
DFF = 4096
EPS = 1e-5
NCORES = 8
ENG = ("pe", "act", "dve", "pool", "sp")


class Prog:
    def __init__(self, nc, stack):
        self.nc = nc
        self.stack = stack
        self.sem = {}
        self.cnt = {}
        self.waited = {e: {} for e in ENG}
        self.q = {e: [] for e in ENG}
        for e in ("pe", "act", "dve", "pool"):
            self._mk(e)

    def _mk(self, key):
        self.sem[key] = self.stack.enter_context(self.nc.semaphore("s_" + key))
        self.cnt[key] = 0

    def _flat(self, deps, out):
        for d in deps:
            if d is None:
                continue
            if isinstance(d, list):
                self._flat(d, out)
            else:
                out.append(d)

    def _wait(self, e, deps):
        fl = []
        self._flat(deps, fl)
        mx = {}
        for d in fl:
            k, v = d[0], d[1]
            if k == e and (e == "pe" or len(d) == 3):
                continue
            if v > mx.get(k, 0):
                mx[k] = v
        for k, v in mx.items():
            if self.waited[e].get(k, 0) >= v:
                continue
            self.waited[e][k] = v
            self.q[e].append(("wait", k, v))

    def op(self, e, name, *args, deps=(), inc=True, **kw):
        self._wait(e, list(deps))
        self.q[e].append(("op", name, args, kw, inc))
        if inc:
            self.cnt[e] += 1
            return (e, self.cnt[e])
        return None

    def dma(self, e, out, in_, semkey, deps=(), **kw):
        if semkey not in self.sem:
            self._mk(semkey)
        self._wait(e, list(deps))
        self.q[e].append(("dma", out, in_, semkey, kw))
        self.cnt[semkey] += 16
        return (semkey, self.cnt[semkey])

    def barrier(self):
        allh = [(k, v) for k, v in self.cnt.items() if v > 0]
        for e in ENG:
            self._wait(e, allh)

    def flush(self):
        with self.nc.Block() as block:
            for e, reg in (("pe", block.tensor), ("act", block.scalar), ("dve", block.vector),
                           ("pool", block.gpsimd), ("sp", block.sync)):
                items = self.q[e]
                self.q[e] = []

                def run(eng, items=items, e=e):
                    for it in items:
                        if it[0] == "wait":
                            eng.wait_ge(self.sem[it[1]], it[2])
                        elif it[0] == "op":
                            _, name, args, kw, inc = it
                            ins = getattr(eng, name)(*args, **kw)
                            if inc:
                                ins.then_inc(self.sem[e], 1)
                        else:
                            _, out, in_, semkey, kw = it
                            eng.dma_start(out=out, in_=in_, **kw).then_inc(self.sem[semkey], 16)

                reg(run)


class Slot:
    def __init__(self, t, idx=0):
        self.t = t
        self.idx = idx
        self.rd = []
        self.wr = []
        self.pending = False

    def wdeps(self):
        d = [(h[0], h[1], "war") for h in self.rd if h is not None] + list(self.wr)
        self.rd = []
        self.wr = []
        return d

    def w(self, h):
        self.wr.append(h)
        self.pending = True

    def rdeps(self):
        return list(self.wr)

    def r(self, h):
        self.rd.append(h)
        self.pending = False


class Ring:
    def __init__(self, ts, check=False):
        self.s = [Slot(t, i) for i, t in enumerate(ts)]
        self.i = -1
        self.check = check

    def next(self):
        self.i = (self.i + 1) % len(self.s)
        if self.check:
            assert not self.s[self.i].pending, "ring slot reused before its consumer was emitted"
        return self.s[self.i]


def build_nc(S, L, dbg=False):
    T = S // 128
    NB = S // 512
    nc = bass.Bass("TRN2", target_bir_lowering=False)
    uid = [0]

    def nm(p="t"):
        uid[0] += 1
        return "%s%d" % (p, uid[0])

    def din(name, shape, dt=F32):
        return nc.dram_tensor(name, shape, dt, kind="ExternalInput").ap()

    def dscr(name, shape, dt):
        return nc.dram_tensor(name, shape, dt, kind="ExternalOutput" if dbg else "Internal").ap()

    x_in = din("x", [S, D])
    pos_row = din("pos_row", [1, S], I32)
    pos_tm = din("pos_tm", [128, T], I32)
    w_in = din("w_in", [L, D, 3488])
    w_uq = din("w_uq", [L, 384, 1024])
    w_ukv = din("w_ukv", [L, 256, 1024])
    w_a = din("w_a", [L, 512, D])
    w_b = din("w_b", [L, 512, D])
    w_o = din("w_o", [L, D, D])
    w_f1 = din("w_f1", [L, D, DFF])
    w_f2 = din("w_f2", [L, DFF, D])
    g_mix = din("g_mix", [L, D])
    g_mlp = din("g_mlp", [L, D])
    g_fin = din("g_fin", [1, D])
    g_q = din("g_q", [L, 384])
    g_kv = din("g_kv", [L, 256])
    sinks = din("sinks", [1, L * 8])
    bias_g = din("bias_g", [4, 128, 512])
    mask_g = din("mask_g", [4, 128, 512])
    ident = din("ident", [128, 128])
    tri = din("tri", [128, 128])
    invf_fm = din("invf_fm", [128, 1])
    out = nc.dram_tensor("out", [S, D], F32, kind="ExternalOutput").ap()

    xres = dscr("xres", [S, D], F32)
    hT_d = dscr("hT_d", [8, 128, S], BF16)
    OaT_d = dscr("OaT_d", [4, 128, S], BF16)
    ObT_d = dscr("ObT_d", [4, 128, S], BF16)
    cqT_d = dscr("cqT_d", [3, 128, S], BF16)
    ckT_d = dscr("ckT_d", [2, 128, S], BF16)
    krT_d = dscr("krT_d", [32, S], BF16)
    qnT_d = dscr("qnT_d", [4, 128, S], BF16)
    qrT_d = dscr("qrT_d", [2, 128, S], BF16)
    knT_d = dscr("knT_d", [4, 128, S], BF16)
    V_d = dscr("V_d", [8, S, 64], BF16)
    cosF_d = dscr("cosF_d", [128, S], F32)
    sinF_d = dscr("sinF_d", [128, S], F32)
    cosT_d = dscr("cosT_d", [128, T * 16], F32)
    sinT_d = dscr("sinT_d", [128, T * 16], F32)

    invf = (10000.0 ** (-(np.arange(16, dtype=np.float32)) / np.float32(16.0))).astype(np.float32)

    with ExitStack() as top:
        P = Prog(nc, top)

        def sbt(shape, dt, st=top):
            return st.enter_context(nc.sbuf_tensor(nm("sb"), shape, dt))

        def pst(shape, dt, st):
            return st.enter_context(nc.psum_tensor(nm("ps"), shape, dt))

        idb = sbt([128, 128], BF16)
        trib = sbt([128, 128], BF16)
        esink = sbt([128, L * 8], F32)
        const_h = []

        def trig(ang, ki, kf, mm, o_sin, o_cos, deps):
            C1 = 6.28125
            C2 = 2 * math.pi - 6.28125
            d = P.op("dve", "tensor_scalar", ki, ang, 1.0 / (2 * math.pi), None, ALU.mult, deps=deps)
            d = P.op("dve", "tensor_copy", kf, ki, deps=[d])
            d = P.op("dve", "scalar_tensor_tensor", ang, kf, -C1, ang, ALU.mult, ALU.add, deps=[d])
            d = P.op("dve", "scalar_tensor_tensor", ang, kf, -C2, ang, ALU.mult, ALU.add, deps=[d])
            d = P.op("dve", "tensor_scalar", mm, ang, math.pi, -2 * math.pi, ALU.is_gt, ALU.mult, deps=[d])
            d = P.op("dve", "tensor_tensor", ang, ang, mm, ALU.add, deps=[d])
            d = P.op("dve", "scalar_tensor_tensor", mm, ang, -1.0, ang, ALU.mult, ALU.max, deps=[d])
            a1 = P.op("act", "activation", o_sin, ang, AF.Sin, deps=[d])
            a2 = P.op("act", "activation", o_cos, mm, AF.Sin, bias=math.pi / 2, scale=-1.0, deps=[d])
            return [a1, a2]

        with ExitStack() as ph:
            CH = min(S, 2048)
            l = P.dma("pool", idb[:], ident, "c0")
            l = P.dma("pool", trib[:], tri, "c0")
            const_h.append(l)
            l3 = P.dma("sp", esink[:], sinks.partition_broadcast(128), "c2")
            a = P.op("act", "activation", esink[:], esink[:], AF.Exp, deps=[l3])
            const_h.append(a)
            pti = sbt([128, T], I32, ph)
            ptf = sbt([128, T], F32, ph)
            angT = sbt([128, T, 16], F32, ph)
            kiT = sbt([128, T, 16], I32, ph)
            kfT = sbt([128, T, 16], F32, ph)
            mmT = sbt([128, T, 16], F32, ph)
            l4 = P.dma("sp", pti[:], pos_tm, "c3")
            d = P.op("dve", "tensor_copy", ptf[:], pti[:], deps=[l4])
            dd = []
            for i in range(16):
                dd.append(P.op("dve", "tensor_scalar", angT[:, :, i], ptf[:], float(invf[i]), None, ALU.mult, deps=[d]))
            sinT0 = sbt([128, T, 16], F32, ph)
            cosT0 = sbt([128, T, 16], F32, ph)
            hh = trig(angT[:], kiT[:], kfT[:], mmT[:], sinT0[:], cosT0[:], dd)
            P.dma("pool", sinT_d, sinT0[:].rearrange("p t f -> p (t f)"), "c6", deps=hh)
            P.dma("pool", cosT_d, cosT0[:].rearrange("p t f -> p (t f)"), "c6", deps=hh)
            ivf = sbt([128, 1], F32, ph)
            l5 = P.dma("sp", ivf[:], invf_fm, "c4")
            pbi = sbt([128, CH], I32, ph)
            pbf = sbt([128, CH], F32, ph)
            ang = sbt([128, CH], F32, ph)
            kif = sbt([128, CH], I32, ph)
            kff = sbt([128, CH], F32, ph)
            mmf = sbt([128, CH], F32, ph)
            osn = sbt([128, CH], F32, ph)
            ocs = sbt([128, CH], F32, ph)
            prev = []
            for ch in range(S // CH):
                l6 = P.dma("sp", pbi[:], pos_row[0:1, ch * CH:(ch + 1) * CH].partition_broadcast(128), "c5", deps=prev)
                d = P.op("dve", "tensor_copy", pbf[:], pbi[:], deps=[l6])
                d = P.op("dve", "tensor_scalar", ang[:], pbf[:], ivf[:], None, ALU.mult, deps=[d, l5] + prev)
                hh = trig(ang[:], kif[:], kff[:], mmf[:], osn[:], ocs[:], [d] + prev)
                s1 = P.dma("pool", sinF_d[:, ch * CH:(ch + 1) * CH], osn[:], "c6", deps=hh)
                s2 = P.dma("pool", cosF_d[:, ch * CH:(ch + 1) * CH], ocs[:], "c6", deps=hh)
                prev = [s1, s2, d] + hh
            P.barrier()
            P.flush()

        def load_w(dst, src2d, ncols_lo, ncols_hi, nchunks, key):
            hs = []
            for c in range(nchunks):
                hs.append(P.dma("pool", dst[:, c, :], src2d[c * 128:(c + 1) * 128, ncols_lo:ncols_hi], key))
            return hs

        def rstd_ops(st, col, n, dep):
            a = P.op("act", "activation", st[:, col + 1:col + 2], st[:, col:col + 1], AF.Ln,
                     bias=EPS, scale=1.0 / n, deps=[dep])
            a = P.op("act", "activation", st[:, col + 1:col + 2], st[:, col + 1:col + 2], AF.Exp,
                     scale=-0.5, deps=[a])
            return a

        def phase_A1(l):
            xsrc = x_in if l == 0 else xres
            with ExitStack() as ph:
                wA = sbt([128, 8, 1440], BF16, ph)
                gmix = sbt([128, D], F32, ph)
                gq = sbt([128, 384], F32, ph)
                gkv = sbt([128, 256], F32, ph)
                xt = Ring([sbt([128, D], F32, ph) for _ in range(4)])
                junk = sbt([128, D], BF16, ph)
                stt = Ring([sbt([128, 8], F32, ph) for _ in range(8)])
                hb = Ring([sbt([128, D], BF16, ph) for _ in range(4)], check=True)
                hT = Ring([sbt([128, 8, 512], BF16, ph) for _ in range(2)])
                qaT = Ring([sbt([128, 4, 512], BF16, ph) for _ in range(2)])
                kaT = sbt([128, 128 + S], BF16, ph)
                va = sbt([128, T, 2, 65], BF16, ph)
                cqn = Ring([sbt([128, 384], BF16, ph) for _ in range(4)], check=True)
                ckn = Ring([sbt([128, 256], BF16, ph) for _ in range(4)], check=True)
                krb = Ring([sbt([128, 32], BF16, ph) for _ in range(4)], check=True)
                krs = sbt([128, 2, 16], F32, ph)
                tmpa = sbt([128, 2, 16], F32, ph)
                tmpb = sbt([128, 2, 16], F32, ph)
                cqT = Ring([sbt([128, 3, 512], BF16, ph) for _ in range(2)])
                ckT = Ring([sbt([128, 2, 512], BF16, ph) for _ in range(2)])
                krT = Ring([sbt([32, 512], BF16, ph) for _ in range(2)])
                pexp = Ring([sbt([128, 512], F32, ph) for _ in range(3)])
                pm = Ring([sbt([128, 512], BF16, ph) for _ in range(16)], check=True)
                oa = Ring([sbt([128, 512], BF16, ph) for _ in range(4)], check=True)
                oaT = Ring([sbt([128, 4, 512], BF16, ph) for _ in range(2)])
                den = Ring([sbt([128, 4], F32, ph) for _ in range(3)])
                tp = Slot(pst([128, 8, 128], BF16, ph))
                tp2 = Slot(pst([128, 8, 128], BF16, ph))
                smA = Slot(pst([128, 512], F32, ph))
                smB = Slot(pst([128, 512], F32, ph))
                fs = Ring([pst([128, 512], F32, ph) for _ in range(2)])
                oacc = [Slot(pst([128, 4, 65], F32, ph)) for _ in range(2)]

                E = sbt([128, 4, 512], F32, ph)
                mk = sbt([128, 4, 512], F32, ph)
                cosT = sbt([128, T, 16], F32, ph)
                sinT = sbt([128, T, 16], F32, ph)
                l1 = P.dma("sp", E[:], bias_g.rearrange("a p n -> p a n"), "c1a")
                l2 = P.dma("sp", mk[:], mask_g.rearrange("a p n -> p a n"), "c1b")
                a = P.op("act", "activation", E[:], E[:], AF.Exp, deps=[l1])
                d = P.op("dve", "tensor_tensor", E[:], E[:], mk[:], ALU.mult, deps=[a, l2])
                l3 = P.dma("sp", cosT[:].rearrange("p t f -> p (t f)"), cosT_d, "c1c")
                l4 = P.dma("sp", sinT[:].rearrange("p t f -> p (t f)"), sinT_d, "c1c")
                l3 = l4
                loc_h = [d, l3, l4]
                wl = load_w(wA, w_in[l], 0, 1440, 8, "wl0")
                gl = [P.dma("sp", gmix[:], g_mix[l:l + 1, :].partition_broadcast(128), "wl1"),
                      P.dma("sp", gq[:], g_q[l:l + 1, :].partition_broadcast(128), "wl1"),
                      P.dma("sp", gkv[:], g_kv[l:l + 1, :].partition_broadcast(128), "wl1")]
                vones = P.op("pool", "memset", va[:, :, :, 64:65], 1.0)
                ka_w = {}
                va_w = {}

                def block_gen(b):
                    hTs = hT.next()
                    hT_wd = hTs.wdeps()
                    cqTs = cqT.next()
                    cqT_wd = cqTs.wdeps()
                    ckTs = ckT.next()
                    ckT_wd = ckTs.wdeps()
                    krTs = krT.next()
                    krT_wd = krTs.wdeps()
                    bs = slice(b * 512, (b + 1) * 512)
                    stt_ = {}

                    def X(i):
                        t = 4 * b + i
                        xs = xt.next()
                        ld = P.dma("sp", xs.t[:], xsrc[t * 128:(t + 1) * 128, :], "xa%d" % xs.idx, deps=xs.wdeps())
                        xs.w(ld)
                        st = stt.next()
                        stw = st.wdeps()
                        a = P.op("act", "activation", junk[:], xs.t[:], AF.Square, accum_out=st.t[:, 0:1],
                                 deps=xs.rdeps() + stw)
                        xs.r(a)
                        a = rstd_ops(st.t, 0, D, a)
                        hs = hb.next()
                        d2 = P.op("dve", "scalar_tensor_tensor", hs.t[:], xs.t[:], st.t[:, 1:2], gmix[:],
                                  ALU.mult, ALU.mult, deps=[a] + gl + hs.wdeps())
                        xs.r(d2)
                        hs.w(d2)
                        st.w(d2)
                        stt_[i] = dict(st=st, hs=hs, d2=d2)

                    def Tr(i):
                        c0 = i * 128
                        hs = stt_[i]["hs"]
                        tpd = tp.wdeps()
                        for c in range(8):
                            pt = P.op("pe", "transpose", tp.t[:, c, :], hs.t[:, c * 128:(c + 1) * 128], idb[:],
                                      deps=hs.rdeps() + tpd + const_h, inc=(c == 7))
                        hs.r(pt)
                        tp.w(pt)
                        ev = P.op("act", "activation", hTs.t[:, :, c0:c0 + 128], tp.t[:], AF.Copy,
                                  deps=tp.rdeps() + hT_wd)
                        tp.r(ev)
                        hTs.w(ev)
                        stt_[i]["ev"] = ev

                    def MA(i):
                        t = 4 * b + i
                        c0 = i * 128
                        st = stt_[i]["st"]
                        ev = stt_[i]["ev"]
                        wdA = smA.wdeps()
                        for c in range(8):
                            mA = P.op("pe", "matmul", smA.t[:, 0:512], hTs.t[:, c, c0:c0 + 128], wA[:, c, 640:1152],
                                      start=(c == 0), stop=(c == 7), deps=[ev] + wl + wdA, inc=(c == 7))
                        smA.w(mA)
                        hTs.r(mA)
                        e1 = P.op("act", "activation", va[:, t, :, 0:64],
                                  smA.t[:, 0:128].rearrange("p (g d) -> p g d", g=2), AF.Copy, deps=[mA, vones])
                        smA.r(e1)
                        va_w[t] = e1
                        a3 = P.op("act", "activation", junk[:, 0:384], smA.t[:, 128:512], AF.Square,
                                  accum_out=st.t[:, 2:3], deps=[mA])
                        smA.r(a3)
                        a3 = rstd_ops(st.t, 2, 384, a3)
                        cq = cqn.next()
                        d4 = P.op("dve", "scalar_tensor_tensor", cq.t[:], smA.t[:, 128:512], st.t[:, 3:4], gq[:],
                                  ALU.mult, ALU.mult, deps=[a3, mA] + cq.wdeps())
                        smA.r(d4)
                        cq.w(d4)
                        stt_[i].update(cq=cq, d4=d4)

                    def M(i):
                        t = 4 * b + i
                        c0 = i * 128
                        st = stt_[i]["st"]
                        ev = stt_[i]["ev"]
                        d2 = stt_[i]["d2"]
                        d4 = stt_[i]["d4"]
                        wdB = smB.wdeps()
                        for c in range(8):
                            mB = P.op("pe", "matmul", smB.t[:, 0:288], hTs.t[:, c, c0:c0 + 128], wA[:, c, 1152:1440],
                                      start=(c == 0), stop=(c == 7), deps=[ev] + wl + wdB, inc=(c == 7))
                        smB.w(mB)
                        hTs.r(mB)
                        a4 = P.op("act", "activation", junk[:, 512:768], smB.t[:, 0:256], AF.Square,
                                  accum_out=st.t[:, 4:5], deps=[mB])
                        smB.r(a4)
                        a4 = rstd_ops(st.t, 4, 256, a4)
                        ck = ckn.next()
                        d5 = P.op("dve", "scalar_tensor_tensor", ck.t[:], smB.t[:, 0:256], st.t[:, 5:6], gkv[:],
                                  ALU.mult, ALU.mult, deps=[a4, mB] + ck.wdeps())
                        smB.r(d5)
                        ck.w(d5)
                        st.r(d2)
                        st.r(d4)
                        st.r(d5)
                        d6 = P.op("dve", "tensor_copy", krs[:], smB.t[:, 256:288].rearrange("p (a b) -> p a b", a=2),
                                  deps=[mB])
                        smB.r(d6)
                        cosb = cosT[:, t, :].unsqueeze(1).to_broadcast([128, 2, 16])
                        sinb = sinT[:, t, :].unsqueeze(1).to_broadcast([128, 2, 16])
                        d7 = P.op("dve", "tensor_tensor", tmpa[:], krs[:], cosb, ALU.mult, deps=[d6] + loc_h)
                        d8 = P.op("dve", "tensor_tensor", tmpb[:], krs[:], sinb, ALU.mult, deps=[d6])
                        kr = krb.next()
                        krw = kr.wdeps()
                        d9 = P.op("dve", "tensor_tensor", kr.t[:, 0:16], tmpa[:, 0, :], tmpb[:, 1, :], ALU.subtract,
                                  deps=[d7, d8] + krw)
                        d10 = P.op("dve", "tensor_tensor", kr.t[:, 16:32], tmpa[:, 1, :], tmpb[:, 0, :], ALU.add,
                                   deps=[d7, d8])
                        kr.w(d9)
                        kr.w(d10)
                        stt_[i].update(ck=ck, kr=kr)

                    def U(i):
                        c0 = i * 128
                        cq, ck, kr = stt_[i]["cq"], stt_[i]["ck"], stt_[i]["kr"]
                        tp2d = tp2.wdeps()
                        for c in range(3):
                            P.op("pe", "transpose", tp2.t[:, c, :], cq.t[:, c * 128:(c + 1) * 128], idb[:],
                                 deps=cq.rdeps() + tp2d, inc=False)
                        for c in range(2):
                            P.op("pe", "transpose", tp2.t[:, 3 + c, :], ck.t[:, c * 128:(c + 1) * 128], idb[:],
                                 deps=ck.rdeps(), inc=False)
                        pt2 = P.op("pe", "transpose", tp2.t[0:32, 5, :], kr.t[:, 0:32], idb[:], deps=kr.rdeps())
                        cq.r(pt2)
                        ck.r(pt2)
                        kr.r(pt2)
                        tp2.w(pt2)
                        e2 = P.op("dve", "tensor_copy", cqTs.t[:, :, c0:c0 + 128], tp2.t[:, 0:3, :], deps=[pt2] + cqT_wd)
                        e3 = P.op("dve", "tensor_copy", ckTs.t[:, :, c0:c0 + 128], tp2.t[:, 3:5, :], deps=[pt2] + ckT_wd)
                        e4 = P.op("dve", "tensor_copy", krTs.t[0:32, c0:c0 + 128], tp2.t[0:32, 5, :], deps=[pt2] + krT_wd)
                        tp2.r(e4)
                        cqTs.w(e2)
                        ckTs.w(e3)
                        krTs.w(e4)

                    qa_ = {}

                    def FM(c4):
                        if "qas" not in qa_:
                            qa_["qas"] = qaT.next()
                            qa_["wd"] = qa_["qas"].wdeps()
                        qas = qa_["qas"]
                        fm = fs.next()
                        fd = fm.wdeps()
                        for c in range(8):
                            m = P.op("pe", "matmul", fm.t[:], wA[:, c, c4 * 128:(c4 + 1) * 128], hTs.t[:, c, :],
                                     start=(c == 0), stop=(c == 7), deps=hTs.rdeps() + fd, inc=(c == 7))
                        fm.w(m)
                        hTs.r(m)
                        if c4 < 4:
                            e = P.op("dve", "tensor_copy", qas.t[:, c4, :], fm.t[:], deps=[m] + qa_["wd"])
                            qas.w(e)
                        else:
                            e = P.op("act", "activation", kaT[:, 128 + b * 512:128 + (b + 1) * 512], fm.t[:], AF.Copy,
                                     deps=[m])
                            ka_w[b] = e
                        fm.r(e)

                    sw = {}

                    def QK(i, gs=(0, 1)):
                        t = 4 * b + i
                        qas = qa_["qas"]
                        tiles = ([t - 1] if t > 0 else []) + [t]
                        pmls = sw.setdefault(i, [])
                        for g in gs:
                            pml = []
                            for ti, kt in enumerate(tiles):
                                kind = 1 if kt == t else 0
                                sc = fs.next()
                                scd = sc.wdeps()
                                kdeps = [ka_w[kt // 4]]
                                m = P.op("pe", "matmul", sc.t[:].rearrange("p (c q) -> p c q", c=4),
                                         kaT[g * 64:(g + 1) * 64, 128 + kt * 128:128 + (kt + 1) * 128],
                                         qas.t[g * 64:(g + 1) * 64, :, i * 128:(i + 1) * 128],
                                         start=True, stop=True, deps=kdeps + qas.rdeps() + scd)
                                sc.w(m)
                                qas.r(m)
                                pe_ = pexp.next()
                                a = P.op("act", "activation", pe_.t[:], sc.t[:], AF.Exp, scale=0.125,
                                         deps=[m] + pe_.wdeps())
                                sc.r(a)
                                pe_.w(a)
                                pms = pm.next()
                                d = P.op("dve", "tensor_tensor", pms.t[:], pe_.t[:], E[:, g * 2 + kind, :], ALU.mult,
                                         deps=[a] + pms.wdeps() + loc_h)
                                pe_.r(d)
                                pms.w(d)
                                pml.append((pms, d, kt))
                            pmls.append(pml)

                    def PV(i):
                        pmls = sw[i]
                        oas = oa.next()
                        oa_wd = oas.wdeps()
                        for g in range(2):
                            pml = pmls[g]
                            oac = oacc[g]
                            oac_wd = oac.wdeps()
                            for c in range(4):
                                for ti, (pms, d, kt) in enumerate(pml):
                                    last = (c == 3 and ti == len(pml) - 1)
                                    mm = P.op("pe", "matmul", oac.t[:, c, :], pms.t[:, c * 128:(c + 1) * 128],
                                              va[:, kt, g, :], start=(ti == 0), stop=(ti == len(pml) - 1),
                                              deps=[d, va_w[kt]] + oac_wd, inc=last)
                            for (pms, d, kt) in pml:
                                pms.r(mm)
                            oac.w(mm)
                            dn = den.next()
                            d1 = P.op("dve", "tensor_tensor", dn.t[:], oac.t[:, :, 64],
                                      esink[:, l * 8 + g * 4:l * 8 + g * 4 + 4], ALU.add,
                                      deps=[mm] + dn.wdeps() + const_h)
                            d2 = P.op("dve", "reciprocal", dn.t[:], dn.t[:], deps=[d1])
                            d3 = P.op("dve", "tensor_tensor",
                                      oas.t[:, g * 256:(g + 1) * 256].rearrange("p (c d) -> p c d", c=4),
                                      oac.t[:, :, 0:64], dn.t[:].unsqueeze(2).to_broadcast([128, 4, 64]), ALU.mult,
                                      deps=[d2] + oa_wd)
                            dn.w(d2)
                            dn.r(d3)
                            oac.r(d3)
                            oas.w(d3)
                        sw[i] = oas

                    def TRo(i):
                        oas = sw[i]
                        if "oaTs" not in qa_:
                            qa_["oaTs"] = oaT.next()
                            qa_["oaT_wd"] = qa_["oaTs"].wdeps()
                        oaTs = qa_["oaTs"]
                        tp2d = tp2.wdeps()
                        for fc in range(4):
                            pt = P.op("pe", "transpose", tp2.t[:, fc, :], oas.t[:, fc * 128:(fc + 1) * 128], idb[:],
                                      deps=oas.rdeps() + tp2d, inc=(fc == 3))
                        oas.r(pt)
                        tp2.w(pt)
                        e = P.op("act", "activation", oaTs.t[:, :, i * 128:(i + 1) * 128], tp2.t[:, 0:4, :], AF.Copy,
                                 deps=[pt] + qa_["oaT_wd"])
                        tp2.r(e)
                        oaTs.w(e)

                    X(0)
                    X(1)
                    yield
                    Tr(0)
                    yield
                    MA(0)
                    yield
                    M(0)
                    X(2)
                    yield
                    Tr(1)
                    yield
                    MA(1)
                    yield
                    M(1)
                    X(3)
                    yield
                    U(0)
                    Tr(2)
                    yield
                    MA(2)
                    yield
                    M(2)
                    yield
                    U(1)
                    Tr(3)
                    yield
                    MA(3)
                    yield
                    M(3)
                    yield
                    U(2)
                    hTs.r(P.dma("pool", hT_d[:, :, bs].rearrange("c p s -> p c s"), hTs.t[:], "sa0_%d" % hTs.idx,
                                deps=hTs.rdeps()))
                    FM(0)
                    yield
                    FM(1)
                    yield
                    U(3)
                    cqTs.r(P.dma("pool", cqT_d[:, :, bs].rearrange("c p s -> p c s"), cqTs.t[:], "sa1_%d" % cqTs.idx,
                                 deps=cqTs.rdeps()))
                    ckTs.r(P.dma("pool", ckT_d[:, :, bs].rearrange("c p s -> p c s"), ckTs.t[:], "sa2_%d" % ckTs.idx,
                                 deps=ckTs.rdeps()))
                    krTs.r(P.dma("pool", krT_d[:, bs], krTs.t[:], "sa3_%d" % krTs.idx, deps=krTs.rdeps()))
                    FM(2)
                    yield
                    FM(3)
                    yield
                    FM(4)
                    QK(0, (0,))
                    yield
                    QK(0, (1,))
                    yield
                    QK(1, (0,))
                    PV(0)
                    yield
                    QK(1, (1,))
                    yield
                    QK(2, (0,))
                    PV(1)
                    TRo(0)
                    yield
                    QK(2, (1,))
                    yield
                    QK(3, (0,))
                    PV(2)
                    TRo(1)
                    yield
                    QK(3, (1,))
                    yield
                    PV(3)
                    TRo(2)
                    yield
                    TRo(3)
                    oaTs = qa_["oaTs"]
                    oaTs.r(P.dma("pool", OaT_d[:, :, bs].rearrange("c p s -> p c s"), oaTs.t[:], "sa4_%d" % oaTs.idx,
                                 deps=oaTs.rdeps()))

                NSTEP = 27
                active = []
                nextb = 0
                while active or nextb < NB:
                    if nextb < NB and (not active or (len(active) < 2 and active[0][1] >= NSTEP // 2)):
                        active.append([block_gen(nextb), 0])
                        nextb += 1
                    for g_ in list(active):
                        try:
                            next(g_[0])
                            g_[1] += 1
                        except StopIteration:
                            active.remove(g_)
                P.barrier()
                P.flush()

        def load_A2_w(l, st):
            wq = sbt([128, 3, 1024], BF16, st)
            wkv = sbt([128, 2, 1024], BF16, st)
            wl = load_w(wq, w_uq[l], 0, 1024, 3, "wl3")
            wl += load_w(wkv, w_ukv[l], 0, 1024, 2, "wl3")
            wn = []
            for c in range(3):
                v = wq[:, c, 768:1024].rearrange("p (h a b) -> p h a b", h=8, a=2)[:, :, 0, :]
                wn.append(P.op("pool", "tensor_scalar", v, v, -1.0, None, ALU.mult, deps=wl))
            return wq, wkv, wl + wn

        def load_E_w(l, st):
            wg = sbt([128, 8, 2048], BF16, st)
            wa = sbt([128, 4, D], BF16, st)
            wb = sbt([128, 4, D], BF16, st)
            wo = sbt([128, 8, D], BF16, st)
            wl = load_w(wg, w_in[l], 1440, 3488, 8, "wl4")
            wl += load_w(wa, w_a[l], 0, D, 4, "wl4")
            wl += load_w(wb, w_b[l], 0, D, 4, "wl4")
            wl += load_w(wo, w_o[l], 0, D, 8, "wl4")
            return wg, wa, wb, wo, wl

        def phase_A2(l, pre):
            with ExitStack() as ph:
                wq, wkv, wl = pre
                cqT = Ring([sbt([128, 3, 512], BF16, ph) for _ in range(2)])
                ckT = Ring([sbt([128, 2, 512], BF16, ph) for _ in range(2)])
                cosb = Ring([sbt([128, 512], F32, ph) for _ in range(2)])
                sinb = Ring([sbt([128, 512], F32, ph) for _ in range(2)])
                qn = Ring([sbt([128, 4, 512], BF16, ph) for _ in range(2)])
                qr = Ring([sbt([128, 2, 512], BF16, ph) for _ in range(2)])
                kn = Ring([sbt([128, 4, 512], BF16, ph) for _ in range(2)])
                vb = Ring([sbt([128, 4, 512], BF16, ph) for _ in range(2)])
                t1 = Ring([sbt([128, 512], F32, ph) for _ in range(2)])
                t2 = Ring([sbt([128, 512], F32, ph) for _ in range(2)])
                psr = Ring([pst([128, 512], F32, ph) for _ in range(8)])
                for b in range(NB):
                    bs = slice(b * 512, (b + 1) * 512)
                    cq = cqT.next()
                    ck = ckT.next()
                    cs = cosb.next()
                    sn = sinb.next()
                    l1 = P.dma("sp", cq.t[:], cqT_d[:, :, bs].rearrange("c p s -> p c s"), "la0_%d" % cq.idx, deps=cq.wdeps())
                    l2 = P.dma("sp", ck.t[:], ckT_d[:, :, bs].rearrange("c p s -> p c s"), "la1_%d" % ck.idx, deps=ck.wdeps())
                    l3 = P.dma("sp", cs.t[:], cosF_d[:, bs], "la2_%d" % cs.idx, deps=cs.wdeps())
                    l4 = P.dma("sp", sn.t[:], sinF_d[:, bs], "la3_%d" % sn.idx, deps=sn.wdeps())
                    cq.w(l1)
                    ck.w(l2)
                    cs.w(l3)
                    sn.w(l4)
                    qns = qn.next()
                    qn_wd = qns.wdeps()
                    for pr in range(4):
                        p_ = psr.next()
                        pd = p_.wdeps()
                        for c in range(3):
                            m = P.op("pe", "matmul", p_.t[:], wq[:, c, pr * 128:(pr + 1) * 128], cq.t[:, c, :],
                                     start=(c == 0), stop=(c == 2), deps=[l1] + wl + pd, inc=(c == 2))
                        p_.w(m)
                        cq.r(m)
                        e = P.op("act", "activation", qns.t[:, pr, :], p_.t[:], AF.Copy, deps=[m] + qn_wd)
                        p_.r(e)
                        qns.w(e)
                    qns.r(P.dma("pool", qnT_d[:, :, bs].rearrange("c p s -> p c s"), qns.t[:], "sb0_%d" % qns.idx,
                                deps=qns.rdeps()))
                    qrs = qr.next()
                    qr_wd = qrs.wdeps()
                    for gp in range(2):
                        pa = psr.next()
                        pad = pa.wdeps()
                        for c in range(3):
                            m1 = P.op("pe", "matmul", pa.t[:], wq[:, c, 512 + gp * 128:512 + (gp + 1) * 128], cq.t[:, c, :],
                                      start=(c == 0), stop=(c == 2), deps=pad, inc=(c == 2))
                        pa.w(m1)
                        pb = psr.next()
                        pbd = pb.wdeps()
                        for c in range(3):
                            m2 = P.op("pe", "matmul", pb.t[:], wq[:, c, 768 + gp * 128:768 + (gp + 1) * 128], cq.t[:, c, :],
                                      start=(c == 0), stop=(c == 2), deps=pbd, inc=(c == 2))
                        pb.w(m2)
                        cq.r(m2)
                        ta = t1.next()
                        tb = t2.next()
                        d1 = P.op("dve", "tensor_tensor", ta.t[:], pa.t[:], cs.t[:], ALU.mult, deps=[m1, l3] + ta.wdeps())
                        d2 = P.op("dve", "tensor_tensor", tb.t[:], pb.t[:], sn.t[:], ALU.mult, deps=[m2, l4] + tb.wdeps())
                        pa.r(d1)
                        pb.r(d2)
                        cs.r(d1)
                        sn.r(d2)
                        d3 = P.op("pool", "tensor_tensor", qrs.t[:, gp, :], ta.t[:], tb.t[:], ALU.add,
                                  deps=[d1, d2] + qr_wd)
                        ta.r(d3)
                        tb.r(d3)
                        qrs.w(d3)
                    qrs.r(P.dma("pool", qrT_d[:, :, bs].rearrange("c p s -> p c s"), qrs.t[:], "sb1_%d" % qrs.idx,
                                deps=qrs.rdeps()))
                    kns = kn.next()
                    kn_wd = kns.wdeps()
                    for pr in range(4):
                        p_ = psr.next()
                        pd = p_.wdeps()
                        for c in range(2):
                            m = P.op("pe", "matmul", p_.t[:], wkv[:, c, pr * 128:(pr + 1) * 128], ck.t[:, c, :],
                                     start=(c == 0), stop=(c == 1), deps=[l2] + pd, inc=(c == 1))
                        p_.w(m)
                        e = P.op("dve", "tensor_copy", kns.t[:, pr, :], p_.t[:], deps=[m] + kn_wd)
                        p_.r(e)
                        kns.w(e)
                    kns.r(P.dma("pool", knT_d[:, :, bs].rearrange("c p s -> p c s"), kns.t[:], "sb2_%d" % kns.idx,
                                deps=kns.rdeps()))
                    vbs = vb.next()
                    vb_wd = vbs.wdeps()
                    for i in range(4):
                        p_ = psr.next()
                        pd = p_.wdeps()
                        for c in range(2):
                            m = P.op("pe", "matmul", p_.t[:], ck.t[:, c, i * 128:(i + 1) * 128], wkv[:, c, 512:1024],
                                     start=(c == 0), stop=(c == 1), deps=pd, inc=(c == 1))
                        p_.w(m)
                        ck.r(m)
                        e = P.op("act", "activation", vbs.t[:, i, :], p_.t[:], AF.Copy, deps=[m] + vb_wd)
                        p_.r(e)
                        vbs.w(e)
                    vrd = vbs.rdeps()
                    for i in range(4):
                        t = 4 * b + i
                        vbs.r(P.dma("pool", V_d[:, t * 128:(t + 1) * 128, :].rearrange("h p d -> p h d"),
                                    vbs.t[:, i, :].rearrange("p (h d) -> p h d", h=8), "sb3_%d" % vbs.idx, deps=vrd))
                P.barrier()
                P.flush()

        def phase_C(l):
            with ExitStack() as ph:
                qT = [sbt([96, S], BF16, ph) for _ in range(2)]
                kT = [sbt([96, S], BF16, ph) for _ in range(2)]
                V = [sbt([128, T, 128], BF16, ph) for _ in range(2)]
                hslot = [Slot(None, i) for i in range(2)]
                ptr = Ring([sbt([128, 512], BF16, ph) for _ in range(6)])
                recr = Ring([sbt([64, 512], F32, ph) for _ in range(2)])
                onr = Ring([sbt([64, 512], BF16, ph) for _ in range(2)])
                sps = Ring([pst([128, 512], F32, ph) for _ in range(5)])
                ops_ = Ring([pst([128, 512], F32, ph) for _ in range(3)])
                vo = [P.op("pool", "memset", V[i][:, :, 64:128], 1.0) for i in range(2)]
                scale = 1.0 / math.sqrt(96.0)
                items = []
                for h in range(8):
                    for Q in range(NB):
                        nj = 4 * Q + 4
                        for j in range(nj):
                            items.append((h, Q, j, nj))
                hld = {}
                st1 = {}
                grp = {}
                pending_epi = []

                def head_load(h):
                    sl = hslot[h % 2]
                    wd = sl.wdeps()
                    k = "lc%d" % (h % 2)
                    q_, k_, v_ = qT[h % 2], kT[h % 2], V[h % 2]
                    P.dma("sp", q_[0:32, :], qrT_d[h // 4, (h % 4) * 32:(h % 4 + 1) * 32, :], k, deps=wd)
                    P.dma("sp", q_[32:96, :], qnT_d[h // 2, (h % 2) * 64:(h % 2 + 1) * 64, :], k)
                    P.dma("sp", k_[0:32, :], krT_d[:, :], k)
                    P.dma("sp", k_[32:96, :], knT_d[h // 2, (h % 2) * 64:(h % 2 + 1) * 64, :], k)
                    hh = P.dma("sp", v_[:, :, 0:64], V_d[h].rearrange("(t p) d -> p t d", p=128), k, deps=[vo[h % 2]])
                    sl.w(hh)
                    hld[h] = hh

                def stage1(i):
                    h, Q, j, nj = items[i]
                    r = j - 4 * Q if j >= 4 * Q else None
                    q0 = r * 128 if r else 0
                    s = sps.next()
                    m = P.op("pe", "matmul", s.t[:, q0:512], kT[h % 2][:, j * 128:(j + 1) * 128],
                             qT[h % 2][:, Q * 512 + q0:(Q + 1) * 512], start=True, stop=True,
                             deps=[hld[h]] + s.wdeps())
                    hslot[h % 2].r(m)
                    s.w(m)
                    p = ptr.next()
                    a = P.op("act", "activation", p.t[:, q0:512], s.t[:, q0:512], AF.Exp, scale=scale,
                             deps=[m] + p.wdeps())
                    s.r(a)
                    p.w(a)
                    if r is not None:
                        d = P.op("dve", "tensor_tensor", p.t[:, q0:q0 + 128], p.t[:, q0:q0 + 128], trib[:], ALU.mult,
                                 deps=[a] + const_h)
                        p.w(d)
                    st1[i] = (p, q0)

                def stage2(i):
                    h, Q, j, nj = items[i]
                    p, q0 = st1.pop(i)
                    if j == 0:
                        o = ops_.next()
                        grp[(h, Q)] = (o, o.wdeps())
                    o, owd = grp[(h, Q)]
                    mm = P.op("pe", "matmul", o.t[:, q0:512], V[h % 2][:, j, :], p.t[:, q0:512],
                              start=(j == 0), stop=(j == nj - 1), deps=p.rdeps() + owd)
                    p.r(mm)
                    hslot[h % 2].r(mm)
                    if j == nj - 1:
                        o.w(mm)
                        rc = recr.next()
                        on = onr.next()
                        d2 = P.op("dve", "reciprocal", rc.t[:], o.t[64:128, :], deps=[mm] + rc.wdeps())
                        d3 = P.op("dve", "tensor_tensor", on.t[:], o.t[0:64, :], rc.t[:], ALU.mult,
                                  deps=[d2] + on.wdeps())
                        rc.r(d3)
                        o.r(d3)
                        on.w(d3)
                        on.r(P.dma("pool", ObT_d[h // 2, (h % 2) * 64:(h % 2 + 1) * 64, Q * 512:(Q + 1) * 512], on.t[:],
                                   "sc%d" % on.idx, deps=[d3]))
                        del grp[(h, Q)]

                LOOK = 3
                n = len(items)
                head_load(0)
                head_load(1)
                for i in range(n + LOOK):
                    if i < n:
                        stage1(i)
                    if i - LOOK >= 0:
                        stage2(i - LOOK)
                        h_, Q_, j_, nj_ = items[i - LOOK]
                        if Q_ == NB - 1 and j_ == nj_ - 1 and h_ + 2 < 8:
                            head_load(h_ + 2)
                P.barrier()
                P.flush()

        def phase_E(l, pre):
            xsrc = x_in if l == 0 else xres
            with ExitStack() as ph:
                wg, wa, wb, wo, wl = pre
                hT = Ring([sbt([128, 8, 512], BF16, ph) for _ in range(2)])
                oaT = Ring([sbt([128, 4, 512], BF16, ph) for _ in range(2)])
                obT = Ring([sbt([128, 4, 512], BF16, ph) for _ in range(2)])
                xt = Ring([sbt([128, D], F32, ph) for _ in range(4)])
                sg = Ring([sbt([128, 512], F32, ph) for _ in range(4)])
                mr = Ring([sbt([128, 512], F32, ph) for _ in range(4)])
                mT = Ring([sbt([128, 8, 512], BF16, ph) for _ in range(2)])
                psg = Ring([pst([128, 512], F32, ph) for _ in range(3)])
                psy = Ring([pst([128, 512], F32, ph) for _ in range(3)])
                pso = Ring([pst([128, 512], F32, ph) for _ in range(2)])
                for b in range(NB):
                    bs = slice(b * 512, (b + 1) * 512)
                    hs = hT.next()
                    oas = oaT.next()
                    obs = obT.next()
                    l1 = P.dma("sp", hs.t[:], hT_d[:, :, bs].rearrange("c p s -> p c s"), "le0_%d" % hs.idx, deps=hs.wdeps())
                    l2 = P.dma("sp", oas.t[:], OaT_d[:, :, bs].rearrange("c p s -> p c s"), "le1_%d" % oas.idx, deps=oas.wdeps())
                    l3 = P.dma("sp", obs.t[:], ObT_d[:, :, bs].rearrange("c p s -> p c s"), "le2_%d" % obs.idx, deps=obs.wdeps())
                    hs.w(l1)
                    oas.w(l2)
                    obs.w(l3)
                    ms = mT.next()
                    m_wd = ms.wdeps()
                    for dc in range(8):
                        halves = []
                        for br, (wbr, oT, lo, goff) in enumerate(((wa, oas, l2, 0), (wb, obs, l3, 1024))):
                            pg = psg.next()
                            pgd = pg.wdeps()
                            for c in range(8):
                                m1 = P.op("pe", "matmul", pg.t[:], wg[:, c, goff + dc * 128:goff + (dc + 1) * 128],
                                          hs.t[:, c, :], start=(c == 0), stop=(c == 7), deps=[l1] + wl + pgd,
                                          inc=(c == 7))
                            pg.w(m1)
                            py = psy.next()
                            pyd = py.wdeps()
                            for c in range(4):
                                m2 = P.op("pe", "matmul", py.t[:], wbr[:, c, dc * 128:(dc + 1) * 128], oT.t[:, c, :],
                                          start=(c == 0), stop=(c == 3), deps=[lo] + pyd, inc=(c == 3))
                            py.w(m2)
                            hs.r(m1)
                            oT.r(m2)
                            s_ = sg.next()
                            a = P.op("act", "activation", s_.t[:], pg.t[:], AF.Sigmoid, deps=[m1] + s_.wdeps())
                            pg.r(a)
                            s_.w(a)
                            m_ = mr.next()
                            d = P.op("dve", "tensor_tensor", m_.t[:], s_.t[:], py.t[:], ALU.mult,
                                     deps=[a, m2] + m_.wdeps())
                            s_.r(d)
                            py.r(d)
                            m_.w(d)
                            halves.append((m_, d))
                        (ma, da), (mb_, db) = halves
                        d3 = P.op("pool", "tensor_tensor", ms.t[:, dc, :], ma.t[:], mb_.t[:], ALU.add,
                                  deps=[da, db] + m_wd)
                        ma.r(d3)
                        mb_.r(d3)
                        ms.w(d3)
                    for i in range(4):
                        t = 4 * b + i
                        xs = xt.next()
                        lx = P.dma("sp", xs.t[:], xsrc[t * 128:(t + 1) * 128, :], "le3_%d" % xs.idx, deps=xs.wdeps())
                        xs.w(lx)
                        dh = []
                        for hf in range(2):
                            po = pso.next()
                            pod = po.wdeps()
                            for c in range(8):
                                m = P.op("pe", "matmul", po.t[:], ms.t[:, c, i * 128:(i + 1) * 128],
                                         wo[:, c, hf * 512:(hf + 1) * 512], start=(c == 0), stop=(c == 7),
                                         deps=ms.rdeps() + pod, inc=(c == 7))
                            po.w(m)
                            ms.r(m)
                            d = P.op("dve", "tensor_tensor", xs.t[:, hf * 512:(hf + 1) * 512],
                                     xs.t[:, hf * 512:(hf + 1) * 512], po.t[:], ALU.add, deps=[m, lx])
                            po.r(d)
                            dh.append(d)
                        xs.r(P.dma("pool", xres[t * 128:(t + 1) * 128, :], xs.t[:], "se0_%d" % xs.idx, deps=dh))
                P.barrier()
                P.flush()

        def phase_F(l, last):
            with ExitStack() as ph:
                w1 = sbt([128, 8, DFF], BF16, ph)
                w2 = sbt([128, 32, D], BF16, ph)
                gm = sbt([128, D], F32, ph)
                gf = sbt([128, D], F32, ph) if last else None
                xt = Ring([sbt([128, D], F32, ph) for _ in range(3)])
                junk = sbt([128, D], BF16, ph)
                stt = Ring([sbt([128, 8], F32, ph) for _ in range(4)])
                hb = Ring([sbt([128, D], BF16, ph) for _ in range(2)])
                h2T = Ring([sbt([128, 8, 512], BF16, ph) for _ in range(2)])
                uT = Ring([sbt([128, 32, 512], BF16, ph) for _ in range(1)])
                rl = Ring([sbt([128, 512], F32, ph) for _ in range(2)])
                tp = Slot(pst([128, 8, 128], BF16, ph))
                pu = Ring([pst([128, 512], F32, ph) for _ in range(4)])
                pd_ = Ring([pst([128, 512], F32, ph) for _ in range(3)])
                wl = load_w(w1, w_f1[l], 0, DFF, 8, "wl0")
                wl2 = load_w(w2, w_f2[l], 0, D, 32, "wl2")
                gl = [P.dma("sp", gm[:], g_mlp[l:l + 1, :].partition_broadcast(128), "wl1")]
                if last:
                    gl.append(P.dma("sp", gf[:], g_fin.partition_broadcast(128), "wl1"))
                blk = {}

                def norm_tile(b, i):
                    if i == 0:
                        hs2 = h2T.next()
                        blk[b] = (hs2, hs2.wdeps(), {})
                    t = 4 * b + i
                    xs = xt.next()
                    lx = P.dma("sp", xs.t[:], xres[t * 128:(t + 1) * 128, :], "lf0_%d" % xs.idx, deps=xs.wdeps())
                    xs.w(lx)
                    st = stt.next()
                    a = P.op("act", "activation", junk[:], xs.t[:], AF.Square, accum_out=st.t[:, 0:1],
                             deps=[lx] + st.wdeps())
                    xs.r(a)
                    a = rstd_ops(st.t, 0, D, a)
                    hs = hb.next()
                    d2 = P.op("dve", "scalar_tensor_tensor", hs.t[:], xs.t[:], st.t[:, 1:2], gm[:],
                              ALU.mult, ALU.mult, deps=[a, lx] + gl + hs.wdeps())
                    xs.r(d2)
                    st.r(d2)
                    hs.w(d2)
                    blk[b][2][i] = hs

                def tr_tile(b, i):
                    hs2, h2_wd, hss = blk[b]
                    hs = hss[i]
                    tpd = tp.wdeps()
                    for c in range(8):
                        pt = P.op("pe", "transpose", tp.t[:, c, :], hs.t[:, c * 128:(c + 1) * 128], idb[:],
                                  deps=hs.rdeps() + tpd + const_h, inc=(c == 7))
                    hs.r(pt)
                    tp.w(pt)
                    ev = P.op("act", "activation", hs2.t[:, :, i * 128:(i + 1) * 128], tp.t[:], AF.Copy,
                              deps=[pt] + h2_wd)
                    tp.r(ev)
                    hs2.w(ev)

                for i in range(4):
                    norm_tile(0, i)
                    tr_tile(0, i)
                for b in range(NB):
                    hs2 = blk[b][0]
                    us = uT.next()
                    u_wd = us.wdeps()
                    uw = {}
                    for fc in range(32):
                        p_ = pu.next()
                        ppd = p_.wdeps()
                        for c in range(8):
                            m = P.op("pe", "matmul", p_.t[:], w1[:, c, fc * 128:(fc + 1) * 128], hs2.t[:, c, :],
                                     start=(c == 0), stop=(c == 7), deps=hs2.rdeps() + wl + ppd, inc=(c == 7))
                        p_.w(m)
                        hs2.r(m)
                        r_ = rl.next()
                        a = P.op("act", "activation", r_.t[:], p_.t[:], AF.Relu, deps=[m] + r_.wdeps())
                        p_.r(a)
                        r_.w(a)
                        eng = "pool" if fc % 2 == 0 else "dve"
                        d = P.op(eng, "tensor_tensor", us.t[:, fc, :], r_.t[:], r_.t[:], ALU.mult, deps=[a] + u_wd)
                        r_.r(d)
                        us.w(d)
                        uw[fc] = d
                        if b + 1 < NB and fc == 16:
                            norm_tile(b + 1, 0)
                        if b + 1 < NB and fc == 28:
                            norm_tile(b + 1, 1)
                    for i in range(4):
                        t = 4 * b + i
                        xs = xt.next()
                        lx = P.dma("sp", xs.t[:], xres[t * 128:(t + 1) * 128, :], "lf1_%d" % xs.idx, deps=xs.wdeps())
                        xs.w(lx)
                        dh = []
                        for hf in range(2):
                            k_ = 2 * i + hf
                            if b + 1 < NB:
                                if k_ in (0, 2, 4, 6):
                                    tr_tile(b + 1, k_ // 2)
                                if k_ == 1:
                                    norm_tile(b + 1, 2)
                                if k_ == 3:
                                    norm_tile(b + 1, 3)
                            po = pd_.next()
                            pod = po.wdeps()
                            for fc in range(32):
                                m = P.op("pe", "matmul", po.t[:], us.t[:, fc, i * 128:(i + 1) * 128],
                                         w2[:, fc, hf * 512:(hf + 1) * 512], start=(fc == 0), stop=(fc == 31),
                                         deps=[uw[fc]] + wl2 + pod, inc=(fc == 31))
                            po.w(m)
                            us.r(m)
                            d = P.op("dve", "tensor_tensor", xs.t[:, hf * 512:(hf + 1) * 512],
                                     xs.t[:, hf * 512:(hf + 1) * 512], po.t[:], ALU.add, deps=[m, lx])
                            po.r(d)
                            dh.append(d)
                        if not last:
                            xs.r(P.dma("pool", xres[t * 128:(t + 1) * 128, :], xs.t[:], "sf0_%d" % xs.idx, deps=dh))
                        else:
                            st = stt.next()
                            a = P.op("act", "activation", junk[:], xs.t[:], AF.Square, accum_out=st.t[:, 0:1],
                                     deps=dh + st.wdeps())
                            a = rstd_ops(st.t, 0, D, a)
                            d2 = P.op("dve", "scalar_tensor_tensor", xs.t[:], xs.t[:], st.t[:, 1:2], gf[:],
                                      ALU.mult, ALU.mult, deps=[a] + gl)
                            st.r(d2)
                            xs.r(P.dma("pool", out[t * 128:(t + 1) * 128, :], xs.t[:], "sf0_%d" % xs.idx, deps=[d2]))
                P.barrier()
                P.flush()

        for l in range(L):
            with ExitStack() as sc1:
                pre = load_A2_w(l, sc1)
                phase_A1(l)
                phase_A2(l, pre)
            with ExitStack() as sc2:
                pre = load_E_w(l, sc2)
                phase_C(l)
                phase_E(l, pre)
            phase_F(l, l == L - 1)
    return nc


def _t5_bucket_np(dist):
    n = np.maximum(dist, 0)
    nf = np.maximum(n, 1).astype(np.float32)
    large = 16 + (np.log(nf / np.float32(16)) / np.float32(math.log(128 / 16)) * np.float32(16)).astype(np.int32)
    large = np.minimum(large, 31)
    return np.where(n < 16, n, large)


def prep_inputs(S, L, x, positions, rel_bias_table, norm_mix, w_in, attn_sinks, q_norm, kv_norm,
                w_uq, w_ukv, w_branch_a, w_branch_b, w_out, norm_mlp, w_ff1, w_ff2, norm_final):
    f = lambda a: np.ascontiguousarray(np.asarray(a, dtype=np.float32))
    w_in = f(w_in)
    perm = []
    for c in range(4):
        perm += list(range(c * 64, (c + 1) * 64)) + list(range((4 + c) * 64, (5 + c) * 64))
    cols = np.array(perm + list(range(512, 3488)))
    w_in_p = np.ascontiguousarray(w_in[:, :, cols])
    w_uq = f(w_uq)
    nope = np.concatenate([np.arange(h * 96, h * 96 + 64) for h in range(8)])
    rope = np.concatenate([np.arange(h * 96 + 64, h * 96 + 96) for h in range(8)])
    rot = np.concatenate([np.concatenate([np.arange(h * 96 + 80, h * 96 + 96), np.arange(h * 96 + 64, h * 96 + 80)])
                          for h in range(8)])
    w_uq_p = np.ascontiguousarray(w_uq[:, :, np.concatenate([nope, rope, rot])])
    w_ukv = f(w_ukv)
    kn = np.concatenate([np.arange(h * 128, h * 128 + 64) for h in range(8)])
    vv = np.concatenate([np.arange(h * 128 + 64, h * 128 + 128) for h in range(8)])
    w_ukv_p = np.ascontiguousarray(w_ukv[:, :, np.concatenate([kn, vv])])
    tab = f(rel_bias_table)
    kk = np.arange(128)[:, None]
    qq = np.arange(128)[None, :]
    bias_g = np.zeros((4, 128, 4, 128), np.float32)
    mask_g = np.zeros((4, 128, 4, 128), np.float32)
    for g in range(2):
        for tile in range(2):
            dist = (128 + qq - kk) if tile == 0 else (qq - kk)
            valid = (dist >= 0) & (dist < 128)
            bk = _t5_bucket_np(dist)
            for c in range(4):
                bias_g[g * 2 + tile, :, c, :] = tab[bk, g * 4 + c]
                mask_g[g * 2 + tile, :, c, :] = valid
    common = {
        "w_in": w_in_p, "w_uq": w_uq_p, "w_ukv": w_ukv_p,
        "w_a": f(w_branch_a), "w_b": f(w_branch_b), "w_o": f(w_out), "w_f1": f(w_ff1), "w_f2": f(w_ff2),
        "g_mix": f(norm_mix), "g_mlp": f(norm_mlp), "g_fin": f(norm_final).reshape(1, D),
        "g_q": f(q_norm), "g_kv": f(kv_norm), "sinks": f(attn_sinks).reshape(1, L * 8),
        "bias_g": bias_g.reshape(4, 128, 512), "mask_g": mask_g.reshape(4, 128, 512),
        "ident": np.eye(128, dtype=np.float32),
        "tri": (kk <= qq).astype(np.float32),
        "invf_fm": (10000.0 ** (-((np.arange(128) % 16).astype(np.float32)) / np.float32(16.0))).astype(np.float32).reshape(128, 1),
    }
    x = f(x)
    positions = np.asarray(positions).astype(np.int32)
    maps = []
    for b in range(x.shape[0]):
        m = dict(common)
        m["x"] = np.ascontiguousarray(x[b])
        m["pos_row"] = np.ascontiguousarray(positions[b].reshape(1, S))
        m["pos_tm"] = np.ascontiguousarray(positions[b].reshape(S // 128, 128).T)
        maps.append(m)
    return maps


_NC_CACHE = {}


def kernel(**inputs):
    x = np.asarray(inputs["x"])
    B, S, _ = x.shape
    L = np.asarray(inputs["w_in"]).shape[0]
    maps = prep_inputs(S, L, **inputs)
    key = (S, L)
    if key not in _NC_CACHE:
        _NC_CACHE[key] = build_nc(S, L)
    nc = _NC_CACHE[key]
    res = run_bass_kernel_spmd(nc, maps, core_ids=list(range(B)))
    return np.stack([np.asarray(r["out"], dtype=np.float32) for r in res.results], axis=0)
```
